# Optimizing a Trainium2 kernel written in Bass

```python
import jax, jax.numpy as jnp
from jax import lax
import numpy as np

D_MODEL = 1024
BATCH = 16
SEQ = 2048
DEPTH = 2

A_GROUPS = 8
A_GROUP_DIM = D_MODEL // 16
A_DIM = A_GROUPS * A_GROUP_DIM
B_GROUPS = 8
B_GROUP_DIM = D_MODEL // 16
B_DIM = B_GROUPS * B_GROUP_DIM
MIX_DIM = A_DIM + B_DIM
IN_EVEN = 2 * A_DIM + 3 * B_DIM
A_CONV_WIDTH = 31
B_CONV_WIDTH = 3
CHUNK = 128
C_GROUPS = 8
C_GROUP_DIM = D_MODEL // 8
C_DIM = C_GROUPS * C_GROUP_DIM
D_FF = 4 * D_MODEL
N_EVEN = (DEPTH + 1) // 2
N_ODD = DEPTH // 2
RMS_EPS = 1e-6
LN_EPS = 1e-5

kernel_name = "hybrid_conformer_shortconv_gmlp_trunk"


def rms_norm(x, g):
    xf = x.astype(jnp.float32)
    y = xf * lax.rsqrt(jnp.mean(xf * xf, axis=-1, keepdims=True) + RMS_EPS)
    return (y * g.astype(jnp.float32)).astype(x.dtype)


def layer_norm(x, g, b):
    xf = x.astype(jnp.float32)
    mu = jnp.mean(xf, axis=-1, keepdims=True)
    xc = xf - mu
    var = jnp.mean(xc * xc, axis=-1, keepdims=True)
    y = xc * lax.rsqrt(var + LN_EPS) * g.astype(jnp.float32) + b.astype(jnp.float32)
    return y.astype(x.dtype)


def causal_depthwise_conv(x, w):
    k = w.shape[0]
    return lax.conv_general_dilated(
        x, w[:, None, :].astype(x.dtype), window_strides=(1,), padding=[(k - 1, 0)],
        dimension_numbers=("NWC", "WIO", "NWC"), feature_group_count=x.shape[-1])


def conv_mixers(h, w_in, conv_a_w, conv_a_b, ln_a_g, ln_a_b, conv_b_w, w_out):
    z = h @ w_in
    a_val, a_gate, b_gate, c_gate, b_val = jnp.split(
        z, [A_DIM, 2 * A_DIM, 2 * A_DIM + B_DIM, 2 * A_DIM + 2 * B_DIM], axis=-1)
    a = a_val * jax.nn.sigmoid(a_gate)
    a = causal_depthwise_conv(a, conv_a_w) + conv_a_b
    a = jax.nn.silu(layer_norm(a, ln_a_g, ln_a_b))
    bo = b_gate * causal_depthwise_conv(c_gate * b_val, conv_b_w)
    return jnp.concatenate([a, bo], axis=-1) @ w_out


def chunked_spatial_gating(h, w_in, b_in, ln_v_g, ln_v_b, w_s, b_s, w_out):
    z = jax.nn.gelu(h @ w_in + b_in, approximate=False)
    u, v = jnp.split(z, 2, axis=-1)
    v = layer_norm(v, ln_v_g, ln_v_b)
    bsz, s, _ = v.shape
    vc = v.reshape(bsz, s // CHUNK, CHUNK, C_GROUPS, C_GROUP_DIM)
    mask = jnp.tril(jnp.ones((CHUNK, CHUNK), dtype=bool))
    ws = jnp.where(mask[None], w_s, 0.0).astype(v.dtype)
    sv = jnp.einsum("gts,bnsgc->bntgc", ws, vc) + b_s.T[None, None, :, :, None].astype(v.dtype)
    y = u * sv.reshape(bsz, s, C_DIM)
    return y @ w_out


def squared_relu_mlp(h, w1, w2):
    a = jax.nn.relu(h @ w1)
    return (a * a) @ w2


def setup_inputs(seed: int = 0) -> dict:
    key = jax.random.key(seed)
    ks = iter(jax.random.split(key, 32))

    def nrm(shape, scale):
        return jax.random.normal(next(ks), shape, jnp.float32) * scale

    def gain(shape):
        return 1.0 + nrm(shape, 0.02)

    d = D_MODEL
    return {
        "x": nrm((BATCH, SEQ, d), 1.0),
        "ev_norm_g": gain((N_EVEN, d)),
        "ev_w_in": nrm((N_EVEN, d, IN_EVEN), d ** -0.5),
        "ev_conv_a_w": nrm((N_EVEN, A_CONV_WIDTH, A_DIM), A_CONV_WIDTH ** -0.5),
        "ev_conv_a_b": nrm((N_EVEN, A_DIM), 0.02),
        "ev_ln_a_g": gain((N_EVEN, A_DIM)),
        "ev_ln_a_b": nrm((N_EVEN, A_DIM), 0.02),
        "ev_conv_b_w": nrm((N_EVEN, B_CONV_WIDTH, B_DIM), B_CONV_WIDTH ** -0.5),
        "ev_w_out": nrm((N_EVEN, MIX_DIM, d), MIX_DIM ** -0.5),
        "od_norm_g": gain((N_ODD, d)),
        "od_w_in": nrm((N_ODD, d, 2 * C_DIM), d ** -0.5),
        "od_b_in": nrm((N_ODD, 2 * C_DIM), 0.02),
        "od_ln_v_g": gain((N_ODD, C_DIM)),
        "od_ln_v_b": nrm((N_ODD, C_DIM), 0.02),
        "od_w_s": nrm((N_ODD, C_GROUPS, CHUNK, CHUNK), CHUNK ** -0.5),
        "od_b_s": gain((N_ODD, C_GROUPS, CHUNK)),
        "od_w_out": nrm((N_ODD, C_DIM, d), C_DIM ** -0.5),
        "mlp_norm_g": gain((DEPTH, d)),
        "mlp_w1": nrm((DEPTH, d, D_FF), d ** -0.5),
        "mlp_w2": nrm((DEPTH, D_FF, d), D_FF ** -0.5),
        "final_norm_g": gain((d,)),
    }


def reference(x, ev_norm_g, ev_w_in, ev_conv_a_w, ev_conv_a_b, ev_ln_a_g, ev_ln_a_b,
              ev_conv_b_w, ev_w_out, od_norm_g, od_w_in, od_b_in, od_ln_v_g, od_ln_v_b,
              od_w_s, od_b_s, od_w_out, mlp_norm_g, mlp_w1, mlp_w2, final_norm_g):
    h = x
    for i in range(DEPTH):
        j = i // 2
        if i % 2 == 0:
            h = h + conv_mixers(rms_norm(h, ev_norm_g[j]), ev_w_in[j], ev_conv_a_w[j],
                                ev_conv_a_b[j], ev_ln_a_g[j], ev_ln_a_b[j],
                                ev_conv_b_w[j], ev_w_out[j])
        else:
            h = h + chunked_spatial_gating(rms_norm(h, od_norm_g[j]), od_w_in[j], od_b_in[j],
                                           od_ln_v_g[j], od_ln_v_b[j], od_w_s[j],
                                           od_b_s[j], od_w_out[j])
        h = h + squared_relu_mlp(rms_norm(h, mlp_norm_g[i]), mlp_w1[i], mlp_w2[i])
    return rms_norm(h, final_norm_g)
```

```python
import contextlib
import numpy as np
import concourse.bass as bass
import concourse.mybir as mybir
from concourse.bass_utils import run_bass_kernel_spmd

F32 = mybir.dt.float32
BF16 = mybir.dt.bfloat16
AF = mybir.ActivationFunctionType
ALU = mybir.AluOpType

ENGS = ("pe", "act", "dve", "pool", "sp")
LIM = 240
DLIM = 15

NCORES = 8
TOK = 4096
T = 1024
NTB = 8
NSLOT = 4
NDG = 12
RMS_EPS = 1e-6
LN_EPS = 1e-5


class _Op:
    __slots__ = ("eng", "fn", "reads", "writes", "dma", "src", "pos", "waits", "signal", "sigval")

    def __init__(self, eng, fn, reads, writes, dma):
        self.eng, self.fn, self.reads, self.writes, self.dma = eng, fn, reads, writes, dma
        self.waits = []
        self.signal = False
        self.sigval = 0


class Prog:
    def __init__(self):
        self.ops = []
        self.state = {}
        self.region_init = {}
        self.eng_count = {e: 0 for e in ENGS}
        self.dma_count = {}
        self.waited = {e: {} for e in ENGS}
        self.by_src = {}

    def _st(self, k):
        st = self.state.get(k)
        if st is None:
            st = [None, dict(self.region_init.get(k[0], {}))]
            self.state[k] = st
        return st

    def fence(self, region):
        merged = dict(self.region_init.get(region, {}))
        for k in [k for k in self.state if k[0] == region]:
            w, rd = self.state.pop(k)
            if w is not None:
                merged[w[0]] = max(merged.get(w[0], 0), w[1])
            for s, p in rd.items():
                merged[s] = max(merged.get(s, 0), p)
        self.region_init[region] = merged

    def op(self, eng, fn, reads=(), writes=(), dma=None):
        o = _Op(eng, fn, tuple(reads), tuple(writes), dma)
        deps = {}

        def need(src, pos):
            if deps.get(src, 0) < pos:
                deps[src] = pos

        for k in o.reads:
            st = self._st(k)
            if st[0] is not None:
                need(*st[0])
        for k in o.writes:
            st = self._st(k)
            if st[0] is not None:
                need(*st[0])
            for s, p in st[1].items():
                need(s, p)
        snap = dict(self.dma_count)
        if dma is not None:
            src = ("dma", dma)
            self.dma_count[dma] = self.dma_count.get(dma, 0) + 1
            pos = self.dma_count[dma]
        else:
            src = eng
            self.eng_count[eng] += 1
            pos = self.eng_count[eng]
        o.src, o.pos = src, pos
        self.by_src[(src, pos)] = o
        for k in o.reads:
            st = self._st(k)
            if st[1].get(src, 0) < pos:
                st[1][src] = pos
        for k in o.writes:
            st = self._st(k)
            st[0] = (src, pos)
            st[1] = {}
        wt = self.waited[eng]
        for s, p in deps.items():
            if s == "pe" and eng == "pe" and dma is None:
                continue
            if isinstance(s, tuple):
                p = snap[s[1]]
            prev = wt.get(s, 0)
            if prev >= p:
                continue
            wt[s] = p
            o.waits.append((s, p, prev))
            if not isinstance(s, tuple):
                self.by_src[(s, p)].signal = True
        self.ops.append(o)
        return o

    def emit(self, nc, final_waits=()):
        cnt = {e: 0 for e in ENGS}
        for o in self.ops:
            if o.dma is None and o.signal:
                cnt[o.eng] += 1
                o.sigval = cnt[o.eng]
        with contextlib.ExitStack() as es:
            sems = {}
            nsem = 0
            for e in ENGS:
                for ep in range((cnt[e] + LIM - 1) // LIM):
                    sems[(e, ep)] = es.enter_context(nc.semaphore(f"s_{e}_{ep}"))
                    nsem += 1
            for d, n in self.dma_count.items():
                for ep in range((n + DLIM - 1) // DLIM):
                    sems[(("dma", d), ep)] = es.enter_context(nc.semaphore(f"d_{d}_{ep}"))
                    nsem += 1
            self.nsem = nsem
            block = es.enter_context(nc.Block())
            regs = {"pe": block.tensor, "act": block.scalar, "dve": block.vector,
                    "pool": block.gpsimd, "sp": block.sync}

            def dma_wait(engine, s, p, prev):
                for ep in range(prev // DLIM, (p - 1) // DLIM + 1):
                    hi = min(p, (ep + 1) * DLIM) - ep * DLIM
                    engine.wait_ge(sems[(s, ep)], 16 * hi)

            for e in ENGS:
                my = [o for o in self.ops if o.eng == e]
                if not my and not (e == "sp" and final_waits):
                    continue

                def body(engine, my=my, e=e):
                    for o in my:
                        for s, p, prev in o.waits:
                            if isinstance(s, tuple):
                                dma_wait(engine, s, p, prev)
                            else:
                                c = self.by_src[(s, p)].sigval
                                engine.wait_ge(sems[(s, (c - 1) // LIM)], (c - 1) % LIM + 1)
                        ins = o.fn(engine)
                        if o.dma is not None:
                            ins.then_inc(sems[(o.src, (o.pos - 1) // DLIM)], 16)
                        elif o.signal:
                            ins.then_inc(sems[(e, (o.sigval - 1) // LIM)], 1)
                    if e == "sp":
                        for d in final_waits:
                            dma_wait(engine, ("dma", d), self.dma_count[d], 0)

                regs[e](body)
        return cnt


SM_G = 0
SM_CA = 32
SM_CAB = 156
SM_LNG = 160
SM_LNB = 164
SM_CB = 168
SM_BU = 180
SM_LNGV = 188
SM_LNBV = 196
SM_N = 204


def build_program(layers=(0, 1), final_norm=True, n_mt=4):
    nc = bass.Bass("TRN2", target_bir_lowering=False)
    dram = lambda n, s, k="ExternalInput": nc.dram_tensor(n, s, F32, kind=k).ap()
    x = dram("x", [TOK, 1024])
    y = dram("y", [TOK, 1024], "ExternalOutput")
    W = {
        "ev_w_in": dram("ev_w_in", [1024, 2560]), "ev_w_out": dram("ev_w_out", [1024, 1024]),
        "od_w_in": dram("od_w_in", [1024, 2048]), "od_w_out": dram("od_w_out", [1024, 1024]),
        "w1_0": dram("w1_0", [1024, 4096]), "w1_1": dram("w1_1", [1024, 4096]),
        "w2_0": dram("w2_0", [4096, 1024]), "w2_1": dram("w2_1", [4096, 1024]),
    }
    smallp = dram("smallp", [128, SM_N])
    bigc = dram("bigc", [128, 4096])

    P = Prog()
    with contextlib.ExitStack() as es:
        sb = lambda name, shape, dt=F32: es.enter_context(nc.sbuf_tensor(name, shape, dt))
        h = sb("h", [128, NTB, 1024])
        xnT = sb("xnT", [128, 8, T], BF16)
        R = sb("R", [128, 12288])
        TT = sb("TT", [128, 12032])
        ring = [sb(f"ring{i}", [128, 8, 512], BF16) for i in range(NSLOT)]
        xn_tm = [sb(f"xn_tm{i}", [128, 1024], BF16) for i in range(3)]
        junk = [sb(f"junk{i}", [128, 1024], BF16) for i in range(2)]
        identf = sb("identf", [128, 128])
        dg = [sb(f"dg{i}", [128, 128], BF16) for i in range(NDG)]
        sm = sb("sm", [128, SM_N])
        cst = sb("cst", [128, 2048])
        biasT = sb("biasT", [128, 8, 128])
        wsT_bf = sb("wsT_bf", [128, 8, 128], BF16)
        ident = sb("ident", [128, 128], BF16)
        ones_f = sb("ones_f", [128, 128])
        mhalf = sb("mhalf", [128, 512])
        st = sb("st", [128, 64])
        halo_a = sb("halo_a", [128, 4, 30], BF16)
        halo_c = sb("halo_c", [128, 4, 2])
        banks = [es.enter_context(nc.psum_tensor(f"bank{i}", [128, 512], F32)) for i in range(8)]

        def rview(base, off, n, dt=F32):
            nb = n * (2 if dt == BF16 else 4)
            assert off % 4 == 0 and nb % 4 == 0
            ap = base[:, off // 4:(off + nb) // 4]
            return ap.bitcast(BF16) if dt == BF16 else ap

        KB = 1024
        a_conv = rview(R, 0, 4 * T).rearrange("p (m t) -> p m t", m=4)
        cconv = rview(R, 16 * KB, 4 * T).rearrange("p (m t) -> p m t", m=4)
        mixT = rview(R, 32 * KB, 8 * T, BF16).rearrange("p (m t) -> p m t", m=8)
        hid = [rview(R, i * 16 * KB, 8 * T, BF16).rearrange("p (m t) -> p m t", m=8) for i in range(2)]
        uT = rview(R, 0, 8 * T, BF16).rearrange("p (m t) -> p m t", m=8)
        vhat = rview(R, 16 * KB, 8 * 1024, BF16).rearrange("p (b c) -> p b c", b=8)
        yT = rview(R, 32 * KB, 8 * T, BF16).rearrange("p (m t) -> p m t", m=8)
        o = 0

        def talloc(n, dt=F32):
            nonlocal o
            v = rview(TT, o, n, dt)
            o += n * (2 if dt == BF16 else 4)
            assert o <= 12032 * 4, o
            return v
        abuf = [talloc(32 + T, BF16) for _ in range(2)]
        cgbuf = [talloc(2 + T) for _ in range(2)]
        ccB = talloc(T)
        sig = [talloc(512) for _ in range(2)]
        ln_mean = [talloc(512) for _ in range(2)]
        ln_t = [talloc(512) for _ in range(2)]
        ln_rstd = talloc(512)
        dbuf = [talloc(512) for _ in range(2)]
        sqb = [talloc(512) for _ in range(2)]
        o = 0
        rbuf = [talloc(512) for _ in range(3)]
        o = 0
        vtmp = [talloc(512) for _ in range(2)]
        vbuf = [talloc(1024) for _ in range(2)]
        svtmp = [talloc(512) for _ in range(4)]
        o = 8 * KB
        obuf = [talloc(1024) for _ in range(8)]
        o = 0
        wsT_f = talloc(1024).rearrange("p (g t) -> p g t", g=8)
        bs_bc = talloc(1024).rearrange("p (g t) -> p g t", g=8)
        tmpB = talloc(512).rearrange("p (g t) -> p g t", g=4)

        bank_ctr = [0]

        def next_bank():
            b = bank_ctr[0] % 8
            bank_ctr[0] += 1
            return b

        pieces = []

        def plan_pieces():
            for mt in range(n_mt):
                for L in layers:
                    if L == 0:
                        for c in (1, 0, 3, 4, 2):
                            pieces.append(("ev_w_in", 0, c * 512))
                        for d in range(2):
                            pieces.append(("ev_w_out", 0, d * 512))
                    else:
                        for c in (2, 3, 0, 1):
                            pieces.append(("od_w_in", 0, c * 512))
                        for d in range(2):
                            pieces.append(("od_w_out", 0, d * 512))
                    for fq in range(4):
                        pieces.append((f"w1_{L}", 0, (2 * fq) * 512))
                        pieces.append((f"w1_{L}", 0, (2 * fq + 1) * 512))
                        pieces.append((f"w2_{L}", fq * 1024, 0))
                        pieces.append((f"w2_{L}", fq * 1024, 512))
        plan_pieces()
        cur = [0]
        issued = [0]

        def issue(i):
            if i >= len(pieces):
                return
            assert i == issued[0]
            issued[0] += 1
            wn, r0, c0 = pieces[i]
            s = i % NSLOT
            src = W[wn][r0:r0 + 1024, c0:c0 + 512].rearrange("(kc p) n -> p kc n", p=128)
            P.op("pool", lambda e, s=s, src=src: e.dma_start(out=ring[s][:], in_=src),
                 writes=[("w", s)], dma=f"w{s}")

        def take(expect):
            i = cur[0]
            assert pieces[i] == expect, (pieces[i], expect)
            assert i < issued[0], (i, issued[0])
            cur[0] += 1
            return i, i % NSLOT

        def release(i):
            issue(i + NSLOT)

        P.op("sp", lambda e: e.dma_start(out=sm[:], in_=smallp), writes=[("sm",)], dma="c")
        P.op("sp", lambda e: e.dma_start(out=cst[:], in_=bigc[:, 0:2048]), writes=[("cst",)], dma="c")
        for i in range(2):
            issue(i)
        late_issue = list(range(2, NSLOT))
        P.op("pool", lambda e: e.memset(identf[:], 0.0), writes=[("identf",)])
        P.op("pool", lambda e: e.affine_select(out=identf[:], in_=identf[:], pattern=[[-1, 128]],
                                                compare_op=ALU.not_equal, fill=1.0, base=0,
                                                channel_multiplier=1),
             reads=[("identf",)], writes=[("identf",)])
        P.op("pool", lambda e: e.tensor_copy(out=ident[:], in_=identf[:]), reads=[("identf",)], writes=[("ident",)])
        P.op("pool", lambda e: e.memset(ones_f[:], 1.0), writes=[("ones",)])
        P.op("pool", lambda e: e.memset(mhalf[:], -0.5), writes=[("mhalf",)])
        P.op("pool", lambda e: e.memset(halo_a[:], 0.0), writes=[("halo_a", m) for m in range(4)])
        P.op("pool", lambda e: e.memset(halo_c[:], 0.0), writes=[("halo_c", m) for m in range(4)])
        if 1 in layers:
            P.op("sp", lambda e: e.dma_start(out=wsT_f, in_=bigc[:, 2048:3072].rearrange("p (g t) -> p g t", g=8)),
                 writes=[("TT", "wsT_f")], dma="c")
            P.op("sp", lambda e: e.dma_start(out=bs_bc, in_=bigc[:, 3072:4096].rearrange("p (g t) -> p g t", g=8)),
                 writes=[("TT", "bs_bc")], dma="c")
            P.op("pool", lambda e: e.affine_select(out=wsT_f, in_=wsT_f, pattern=[[0, 8], [1, 128]],
                                                    compare_op=ALU.is_ge, fill=0.0, base=0,
                                                    channel_multiplier=-1),
                 reads=[("TT", "wsT_f")], writes=[("TT", "wsT_f")])
            P.op("pool", lambda e: e.tensor_copy(out=wsT_bf[:], in_=wsT_f), reads=[("TT", "wsT_f")], writes=[("wsT_bf",)])
            for hb in range(2):
                b = next_bank()

                def rs_mm(e, hb=hb, b=b):
                    for j in range(4):
                        r = e.matmul(banks[b][:, j * 128:(j + 1) * 128], lhsT=ones_f[:], rhs=wsT_f[:, hb * 4 + j, :],
                                     start=True, stop=True)
                    return r
                P.op("pe", rs_mm, reads=[("TT", "wsT_f"), ("ones",)], writes=[("ps", b)])
                P.op("dve", lambda e, hb=hb, b=b: e.tensor_tensor(
                    out=tmpB, in0=banks[b][:].rearrange("p (g t) -> p g t", g=4),
                    in1=sm[:, SM_LNBV + hb * 4:SM_LNBV + hb * 4 + 4].unsqueeze(2).to_broadcast([128, 4, 128]),
                    op=ALU.mult), reads=[("ps", b), ("sm",)], writes=[("TT", "tmpB")])
                P.op("dve", lambda e, hb=hb: e.tensor_tensor(
                    out=biasT[:, hb * 4:hb * 4 + 4, :], in0=tmpB, in1=bs_bc[:, hb * 4:hb * 4 + 4, :], op=ALU.add),
                    reads=[("TT", "tmpB"), ("TT", "bs_bc")], writes=[("biasT", hb)])
        P.fence("TT")

        def small(eng, fn, reads, writes):
            P.op(eng, fn, reads=reads, writes=writes)

        stc = [0]

        def stcol(n=1):
            c = stc[0]
            if c + n > 64:
                c = 0
            stc[0] = (c + n) % 64
            return c

        def rstd_from_ss(ss_col, out_col, inv_n, eps, rkeys):
            small("dve", lambda e: e.tensor_scalar(out=st[:, out_col:out_col + 1], in0=st[:, ss_col:ss_col + 1],
                                                   scalar1=inv_n, scalar2=eps, op0=ALU.mult, op1=ALU.add),
                  rkeys, [("st", out_col)])
            small("pool", lambda e: e.tensor_tensor(out=st[:, out_col:out_col + 1], in0=st[:, out_col:out_col + 1],
                                                    in1=mhalf[:, 0:1], op=ALU.pow),
                  [("st", out_col), ("mhalf",)], [("st", out_col)])

        def h_keys(tb):
            return [("h", tb, 0), ("h", tb, 1)]

        def batched_rstd(c0, inv_n, eps):
            small("dve", lambda e: e.tensor_scalar(out=st[:, c0 + 8:c0 + 16], in0=st[:, c0:c0 + 8],
                                                   scalar1=inv_n, scalar2=eps, op0=ALU.mult, op1=ALU.add),
                  [("st", c0 + i) for i in range(8)], [("st", c0 + 8 + i) for i in range(8)])
            small("act", lambda e: e.activation(out=st[:, c0 + 8:c0 + 16], in_=st[:, c0 + 8:c0 + 16], func=AF.Sqrt),
                  [("st", c0 + 8 + i) for i in range(8)], [("st", c0 + 8 + i) for i in range(8)])
            small("dve", lambda e: e.reciprocal(out=st[:, c0 + 8:c0 + 16], in_=st[:, c0 + 8:c0 + 16]),
                  [("st", c0 + 8 + i) for i in range(8)], [("st", c0 + 8 + i) for i in range(8)])

        def stat_tb(c0, tb):
            jb = tb % 2
            small("act", lambda e: e.activation(out=junk[jb][:], in_=h[:, tb, :], func=AF.Square,
                                                accum_out=st[:, c0 + tb:c0 + tb + 1]),
                  h_keys(tb), [("st", c0 + tb), ("junk", jb)])

        pre = [None]

        def tail_hook(tb):
            if pre[0] is not None:
                stat_tb(pre[0], tb)

        def norm_stats():
            if pre[0] is None:
                c0 = stcol(16)
                for tb in range(NTB):
                    stat_tb(c0, tb)
            else:
                c0 = pre[0]
                pre[0] = None
            batched_rstd(c0, 1.0 / 1024, RMS_EPS)
            return c0

        LAG = 2
        xn_done = [False]
        early = [None]
        hold_release = [False]
        hold_final = [False]
        held = []
        deferred_tr = []
        fm_since = [0]

        def flush_deferred():
            while deferred_tr:
                nidx, tb = deferred_tr.pop(0)
                tr_op(nidx, tb)

        def xn_op(c0, tb, eng="act"):
            xb = tb % 3
            if eng == "act":
                small("act", lambda e: e.activation(out=xn_tm[xb][:], in_=h[:, tb, :], func=AF.Copy,
                                                    scale=st[:, c0 + 8 + tb:c0 + 9 + tb]),
                      h_keys(tb) + [("st", c0 + 8 + tb)], [("xn_tm", xb)])
            else:
                small("dve", lambda e: e.tensor_scalar(out=xn_tm[xb][:], in0=h[:, tb, :],
                                                       scalar1=st[:, c0 + 8 + tb:c0 + 9 + tb], scalar2=None, op0=ALU.mult),
                      h_keys(tb) + [("st", c0 + 8 + tb)], [("xn_tm", xb)])

        def tr_op(nidx, tb):
            xb = tb % 3
            b = next_bank()
            tbk = banks[b][:].bitcast(BF16)

            def tr(e):
                for kc in range(8):
                    r = e.transpose(tbk[:, kc * 128:(kc + 1) * 128], xn_tm[xb][:, kc * 128:(kc + 1) * 128], ident[:])
                return r
            P.op("pe", tr, reads=[("xn_tm", xb), ("ident",)], writes=[("ps", b)])
            small("dve", lambda e: e.tensor_tensor(
                out=xnT[:, :, tb * 128:(tb + 1) * 128], in0=tbk.rearrange("p (k n) -> p k n", k=8),
                in1=sm[:, SM_G + nidx * 8:SM_G + nidx * 8 + 8].unsqueeze(2).to_broadcast([128, 8, 128]),
                op=ALU.mult), [("ps", b), ("sm",)], [("xnT", tb)])

        def rstd_tb(c0, tb):
            col = st[:, c0 + 8 + tb:c0 + 9 + tb]
            k = [("st", c0 + 8 + tb)]
            small("dve", lambda e: e.tensor_scalar(out=col, in0=st[:, c0 + tb:c0 + tb + 1], scalar1=1.0 / 1024,
                                                   scalar2=RMS_EPS, op0=ALU.mult, op1=ALU.add), [("st", c0 + tb)], k)
            small("act", lambda e: e.activation(out=col, in_=col, func=AF.Sqrt), k, k)
            small("dve", lambda e: e.reciprocal(out=col, in_=col), k, k)

        def norm_to_xnT(nidx):
            if xn_done[0]:
                xn_done[0] = False
                return
            pre_c0 = early[0]
            early[0] = None
            c0 = stcol(16) if pre_c0 is None else pre_c0
            if pre_c0 is None:
                for tb in range(3):
                    stat_tb(c0, tb)
            LATE = 5
            for tb in range(NTB):
                if tb >= LAG:
                    tr_op(nidx, tb - LAG)
                if tb >= LATE:
                    stat_tb(c0, tb)
                if pre_c0 is not None and tb == 0:
                    rstd_rcp(c0, 0)
                else:
                    rstd_tb(c0, tb)
                if tb + 3 < LATE:
                    stat_tb(c0, tb + 3)
                xn_op(c0, tb, "act" if tb % 2 == 0 else "dve")
            for tb in range(NTB - LAG, NTB):
                tr_op(nidx, tb)

        def rstd_aff(c0, tb):
            col = st[:, c0 + 8 + tb:c0 + 9 + tb]
            small("dve", lambda e: e.tensor_scalar(out=col, in0=st[:, c0 + tb:c0 + tb + 1], scalar1=1.0 / 1024,
                                                   scalar2=RMS_EPS, op0=ALU.mult, op1=ALU.add),
                  [("st", c0 + tb)], [("st", c0 + 8 + tb)])

        def rstd_sqrt(c0, tb):
            col = st[:, c0 + 8 + tb:c0 + 9 + tb]
            k = [("st", c0 + 8 + tb)]
            small("act", lambda e: e.activation(out=col, in_=col, func=AF.Sqrt), k, k)

        def rstd_rcp(c0, tb):
            col = st[:, c0 + 8 + tb:c0 + 9 + tb]
            k = [("st", c0 + 8 + tb)]
            small("dve", lambda e: e.reciprocal(out=col, in_=col), k, k)

        def make_tail(kind, nidx=None, mt=None):
            if kind is None:
                return (lambda tb: None), (lambda: None)
            c0 = stcol(16)
            nc0 = None
            if kind == "final" and mt + 1 < n_mt:
                nc0 = stcol(16)
                early[0] = nc0

            def consume(tb):
                rstd_rcp(c0, tb)
                if kind == "norm":
                    xn_op(c0, tb)
                else:
                    final_tb(c0, tb, mt)

            def post(tb):
                if tb >= 2:
                    consume(tb - 2)
                if tb >= 1:
                    rstd_aff(c0, tb - 1)
                    rstd_sqrt(c0, tb - 1)
                stat_tb(c0, tb)
                if kind == "norm" and tb >= 3:
                    tr_op(nidx, tb - 3)
                if nc0 is not None:
                    if tb == 6:
                        stat_tb(nc0, 0)
                        stat_tb(nc0, 1)
                    elif tb == 7:
                        stat_tb(nc0, 2)
                        rstd_aff(nc0, 0)
                        rstd_sqrt(nc0, 0)

            def flush():
                consume(NTB - 2)
                rstd_aff(c0, NTB - 1)
                rstd_sqrt(c0, NTB - 1)
                if kind == "norm":
                    tr_op(nidx, NTB - 3)
                consume(NTB - 1)
                if kind == "norm":
                    for tb in range(NTB - 2, NTB):
                        deferred_tr.append((nidx, tb))
                    fm_since[0] = 0
                    xn_done[0] = True
            return post, flush

        def final_tb(c0, tb, mt):
            dst = y[mt * T + tb * 128:mt * T + (tb + 1) * 128, :]
            small("dve", lambda e: e.scalar_tensor_tensor(
                out=obuf[tb], in0=h[:, tb, :], scalar=st[:, c0 + 8 + tb:c0 + 9 + tb], in1=cst[:, 0:1024],
                op0=ALU.mult, op1=ALU.mult), h_keys(tb) + [("st", c0 + 8 + tb), ("cst",)], [("TT", "obuf", tb)])
            P.op("sp", lambda e: e.dma_start(out=dst, in_=obuf[tb]), reads=[("TT", "obuf", tb)], dma="out")
            if mt + 1 < n_mt:
                load_x(mt + 1, tb)

        def load_x(mt, tb):
            r0 = mt * T
            P.op("sp", lambda e: e.dma_start(out=h[:, tb, :], in_=x[r0 + tb * 128:r0 + (tb + 1) * 128, :]),
                 writes=h_keys(tb), dma=f"x{tb}")

        def tm_pair(pieces2, lhs, keys_fn, tail):
            post, flush = tail
            for tb in range(NTB):
                for d, (i, slot) in enumerate(pieces2):
                    b = tm_group(slot, lhs, keys_fn(tb), tb)
                    add_to_h(b, tb, d)
                post(tb)
            flush()
            for i, slot in pieces2:
                if hold_release[0]:
                    held.append(i)
                else:
                    release(i)
            hold_release[0] = False

        def fm_group(slot, m, s):
            if deferred_tr and (s == 1 or fm_since[0] >= 2):
                flush_deferred()
            fm_since[0] += 1
            b = next_bank()

            def mm(e):
                for kc in range(8):
                    r = e.matmul(banks[b][:], lhsT=ring[slot][:, kc, m * 128:(m + 1) * 128],
                                 rhs=xnT[:, kc, s * 512:(s + 1) * 512], start=(kc == 0), stop=(kc == 7))
                return r
            P.op("pe", mm, reads=[("w", slot)] + [("xnT", tb) for tb in range(4 * s, 4 * s + 4)], writes=[("ps", b)])
            return b

        def tm_group(slot, lhs, lhs_keys, tb):
            b = next_bank()

            def mm(e):
                for kc in range(8):
                    r = e.matmul(banks[b][:], lhsT=lhs(kc, tb), rhs=ring[slot][:, kc, :],
                                 start=(kc == 0), stop=(kc == 7))
                return r
            P.op("pe", mm, reads=[("w", slot)] + lhs_keys, writes=[("ps", b)])
            return b

        def add_to_h(b, tb, d):
            small("dve", lambda e: e.tensor_tensor(out=h[:, tb, d * 512:(d + 1) * 512], in0=banks[b][:],
                                                   in1=h[:, tb, d * 512:(d + 1) * 512], op=ALU.add),
                  [("ps", b), ("h", tb, d)], [("h", tb, d)])

        def out_proj(wname, srcT, key_fn, tail):
            p2 = [take((wname, 0, 0)), take((wname, 0, 512))]
            tm_pair(p2, lambda kc, tb: srcT[:, kc, tb * 128:(tb + 1) * 128], key_fn, tail)

        def mlp(L, tail):
            norm_to_xnT(1 if L == 0 else 3)
            for fq in range(4):
                hq = hid[fq % 2]
                for j in range(2):
                    i, slot = take((f"w1_{L}", 0, (2 * fq + j) * 512))
                    for s in range(2):
                        for m in range(4):
                            b = fm_group(slot, m, s)
                            rb = next_r[0] % 3
                            next_r[0] += 1
                            small("act", lambda e, b=b, rb=rb: e.activation(out=rbuf[rb], in_=banks[b][:], func=AF.Relu),
                                  [("ps", b)], [("TT", "rbuf", rb)])
                            small("pool", lambda e, rb=rb, hq=hq, c=j * 4 + m, s=s: e.tensor_tensor(
                                out=hq[:, c, s * 512:(s + 1) * 512], in0=rbuf[rb], in1=rbuf[rb], op=ALU.mult),
                                [("TT", "rbuf", rb)], [("R", "hid", fq % 2, j * 4 + m, s)])
                    release(i)
                p2 = [take((f"w2_{L}", fq * 1024, 0)), take((f"w2_{L}", fq * 1024, 512))]
                hold_release[0] = hold_final[0] and fq == 3
                tm_pair(p2, lambda kc, tb, hq=hq: hq[:, kc, tb * 128:(tb + 1) * 128],
                        lambda tb, fq=fq: [("R", "hid", fq % 2, c, tb // 4) for c in range(8)],
                        tail if fq == 3 else make_tail(None))
        next_r = [0]
        want_tail = [True]
        dgc = [0]

        def mixer0(mt):
            half = mt % 2
            norm_to_xnT(0)
            while late_issue:
                issue(late_issue.pop(0))
            while held:
                release(held.pop(0))
            ig, sg = take(("ev_w_in", 0, 512))
            iv, sv = take(("ev_w_in", 0, 0))
            def a_glu(m):
                ab = abuf[m % 2]
                abk = ("TT", "abuf", m % 2)
                if half == 0:
                    small("pool", lambda e, ab=ab: e.memset(ab[:, 0:30], 0.0), [], [abk + ("halo",)])
                else:
                    small("pool", lambda e, ab=ab, m=m: e.tensor_copy(out=ab[:, 0:30], in_=halo_a[:, m, :]),
                          [("halo_a", m)], [abk + ("halo",)])
                for s in range(2):
                    bg = fm_group(sg, m, s)
                    bv = fm_group(sv, m, s)
                    sgb = (2 * m + s) % 2
                    small("act", lambda e, bg=bg, sgb=sgb: e.activation(out=sig[sgb], in_=banks[bg][:], func=AF.Sigmoid),
                          [("ps", bg)], [("TT", "sig", sgb)])
                    small("dve", lambda e, bv=bv, sgb=sgb, ab=ab, s=s: e.tensor_tensor(
                        out=ab[:, 30 + s * 512:30 + (s + 1) * 512], in0=banks[bv][:], in1=sig[sgb], op=ALU.mult),
                        [("ps", bv), ("TT", "sig", sgb)], [abk + (s,)])
                allab = [abk + ("halo",), abk + (0,), abk + (1,)]
                small("pool", lambda e, ab=ab, m=m: e.tensor_copy(out=halo_a[:, m, :], in_=ab[:, T:T + 30]),
                      [abk + (1,)], [("halo_a", m)])

            def a_conv31(m):
                ab = abuf[m % 2]
                abk = ("TT", "abuf", m % 2)
                allab = [abk + ("halo",), abk + (0,), abk + (1,)]
                cb0, cb1 = next_bank(), next_bank()
                for k in range(31):
                    j = dgc[0] % NDG
                    dgc[0] += 1
                    small("dve", lambda e, j=j, m=m, k=k: e.tensor_scalar(
                        out=dg[j][:], in0=identf[:], scalar1=sm[:, SM_CA + m * 31 + k:SM_CA + m * 31 + k + 1],
                        scalar2=None, op0=ALU.mult), [("identf",), ("sm",)], [("dg", j)])

                    def tapmm(e, j=j, k=k, ab=ab, cb0=cb0, cb1=cb1):
                        e.matmul(banks[cb0][:], lhsT=dg[j][:], rhs=ab[:, k:k + 512], start=(k == 0), stop=(k == 30))
                        return e.matmul(banks[cb1][:], lhsT=dg[j][:], rhs=ab[:, 512 + k:1024 + k],
                                        start=(k == 0), stop=(k == 30))
                    P.op("pe", tapmm, reads=[("dg", j)] + allab, writes=[("ps", cb0), ("ps", cb1)])
                for s_, cb in ((0, cb0), (1, cb1)):
                    small("act", lambda e, m=m, s_=s_, cb=cb: e.activation(
                        out=a_conv[:, m, s_ * 512:(s_ + 1) * 512], in_=banks[cb][:], func=AF.Identity,
                        bias=sm[:, SM_CAB + m:SM_CAB + m + 1], scale=1.0), [("ps", cb), ("sm",)], [("R", "a_conv", m)])
            a_glu(0)
            for m in range(4):
                if m + 1 < 4:
                    a_glu(m + 1)
                a_conv31(m)
            release(ig)
            release(iv)
            for s in range(2):
                cs = slice(s * 512, (s + 1) * 512)
                b1 = next_bank()

                def mm1(e, b1=b1, cs=cs):
                    for m in range(4):
                        r = e.matmul(banks[b1][:], lhsT=ones_f[:], rhs=a_conv[:, m, cs], start=(m == 0), stop=(m == 3))
                    return r
                P.op("pe", mm1, reads=[("R", "a_conv", m) for m in range(4)] + [("ones",)], writes=[("ps", b1)])
                b2 = next_bank()
                for m in range(4):
                    small("act", lambda e, m=m, cs=cs: e.activation(out=sqb[m % 2], in_=a_conv[:, m, cs], func=AF.Square),
                          [("R", "a_conv", m)], [("TT", "sqb", m % 2)])
                    P.op("pe", lambda e, m=m, b2=b2: e.matmul(banks[b2][:], lhsT=ones_f[:], rhs=sqb[m % 2],
                                                              start=(m == 0), stop=(m == 3)),
                         reads=[("TT", "sqb", m % 2), ("ones",)], writes=[("ps", b2)])
                small("dve", lambda e, b1=b1, s=s: e.tensor_scalar(out=ln_mean[s], in0=banks[b1][:], scalar1=1.0 / 512,
                                                                   scalar2=None, op0=ALU.mult), [("ps", b1)], [("TT", "ln_mean", s)])
                small("dve", lambda e, b2=b2, s=s: e.tensor_scalar(out=ln_t[s], in0=banks[b2][:], scalar1=1.0 / 512,
                                                                   scalar2=None, op0=ALU.mult), [("ps", b2)], [("TT", "ln_t", s)])
            ln_thunks = []

            def lazy(eng, fn, reads, writes):
                ln_thunks.append(lambda: small(eng, fn, reads, writes))

            def drain(k):
                for _ in range(min(k, len(ln_thunks))):
                    ln_thunks.pop(0)()
            for s in range(2):
                cs = slice(s * 512, (s + 1) * 512)
                lazy("dve", lambda e, s=s: e.tensor_tensor(out=ln_rstd, in0=ln_mean[s], in1=ln_mean[s], op=ALU.mult),
                      [("TT", "ln_mean", s)], [("TT", "ln_rstd")])
                lazy("dve", lambda e, s=s: e.scalar_tensor_tensor(out=ln_t[s], in0=ln_t[s], scalar=LN_EPS, in1=ln_rstd,
                                                                  op0=ALU.add, op1=ALU.subtract),
                      [("TT", "ln_t", s), ("TT", "ln_rstd")], [("TT", "ln_t", s)])
                lazy("act", lambda e, s=s: e.activation(out=ln_rstd, in_=ln_t[s], func=AF.Sqrt),
                      [("TT", "ln_t", s)], [("TT", "ln_rstd")])
                lazy("dve", lambda e: e.reciprocal(out=ln_rstd, in_=ln_rstd), [("TT", "ln_rstd")], [("TT", "ln_rstd")])
                for m in range(4):
                    db = m % 2
                    lazy("dve", lambda e, m=m, cs=cs, db=db, s=s: e.tensor_tensor(out=dbuf[db], in0=a_conv[:, m, cs],
                                                                                   in1=ln_mean[s], op=ALU.subtract),
                          [("R", "a_conv", m), ("TT", "ln_mean", s)], [("TT", "dbuf", db)])
                    lazy("dve", lambda e, db=db: e.tensor_tensor(out=dbuf[db], in0=dbuf[db], in1=ln_rstd, op=ALU.mult),
                          [("TT", "dbuf", db), ("TT", "ln_rstd")], [("TT", "dbuf", db)])
                    lazy("act", lambda e, m=m, cs=cs, db=db: e.activation(
                        out=mixT[:, m, cs], in_=dbuf[db], func=AF.Silu, bias=sm[:, SM_LNB + m:SM_LNB + m + 1],
                        scale=sm[:, SM_LNG + m:SM_LNG + m + 1]), [("TT", "dbuf", db), ("sm",)], [("R", "mixT", m, s)])
            ic, sc = take(("ev_w_in", 0, 1536))
            ib, sbv = take(("ev_w_in", 0, 2048))
            for m in range(4):
                cb = cgbuf[m % 2]
                cbk = ("TT", "cgbuf", m % 2)
                if half == 0:
                    small("pool", lambda e, cb=cb: e.memset(cb[:, 0:2], 0.0), [], [cbk + ("halo",)])
                else:
                    small("pool", lambda e, cb=cb, m=m: e.tensor_copy(out=cb[:, 0:2], in_=halo_c[:, m, :]),
                          [("halo_c", m)], [cbk + ("halo",)])
                for s in range(2):
                    bc = fm_group(sc, m, s)
                    bvv = fm_group(sbv, m, s)
                    sgb = (2 * m + s) % 2
                    small("act", lambda e, bc=bc, sgb=sgb: e.activation(out=sig[sgb], in_=banks[bc][:], func=AF.Copy),
                          [("ps", bc)], [("TT", "sig", sgb)])
                    small("dve", lambda e, bvv=bvv, sgb=sgb, cb=cb, s=s: e.tensor_tensor(
                        out=cb[:, 2 + s * 512:2 + (s + 1) * 512], in0=banks[bvv][:], in1=sig[sgb], op=ALU.mult),
                        [("ps", bvv), ("TT", "sig", sgb)], [cbk + (s,)])
                    drain(6)
                allcb = [cbk + ("halo",), cbk + (0,), cbk + (1,)]
                small("pool", lambda e, cb=cb, m=m: e.tensor_copy(out=halo_c[:, m, :], in_=cb[:, T:T + 2]),
                      [cbk + (1,)], [("halo_c", m)])
                wc = lambda m, k: sm[:, SM_CB + m * 3 + k:SM_CB + m * 3 + k + 1]
                small("dve", lambda e, cb=cb, m=m: e.tensor_scalar(out=cconv[:, m, :], in0=cb[:, 2:2 + T], scalar1=wc(m, 2),
                                                                   scalar2=None, op0=ALU.mult),
                      allcb + [("sm",)], [("R", "cconv", m)])
                small("dve", lambda e, cb=cb, m=m: e.scalar_tensor_tensor(out=cconv[:, m, :], in0=cb[:, 1:1 + T], scalar=wc(m, 1),
                                                                          in1=cconv[:, m, :], op0=ALU.mult, op1=ALU.add),
                      allcb + [("sm",), ("R", "cconv", m)], [("R", "cconv", m)])
                small("dve", lambda e, cb=cb, m=m: e.scalar_tensor_tensor(out=cconv[:, m, :], in0=cb[:, 0:T], scalar=wc(m, 0),
                                                                          in1=cconv[:, m, :], op0=ALU.mult, op1=ALU.add),
                      allcb + [("sm",), ("R", "cconv", m)], [("R", "cconv", m)])
            drain(10 ** 6)
            release(ic)
            release(ib)
            i2, s2 = take(("ev_w_in", 0, 1024))
            for m in range(4):
                for s in range(2):
                    b = fm_group(s2, m, s)
                    small("dve", lambda e, b=b, m=m, s=s: e.tensor_tensor(
                        out=mixT[:, 4 + m, s * 512:(s + 1) * 512], in0=banks[b][:], in1=cconv[:, m, s * 512:(s + 1) * 512],
                        op=ALU.mult), [("ps", b), ("R", "cconv", m)], [("R", "mixT", 4 + m, s)])
            release(i2)
            out_proj("ev_w_out", mixT, lambda tb: [("R", "mixT", c, tb // 4) for c in range(8)], make_tail("norm", 1))

        def mixer1(mt):
            norm_to_xnT(2)
            while late_issue:
                issue(late_issue.pop(0))
            while held:
                release(held.pop(0))
            i0, s0 = take(("od_w_in", 0, 1024))
            i1, s1 = take(("od_w_in", 0, 1536))
            def v_stage(tb):
                if tb >= 3:
                    flush_deferred()
                vb = tb % 2
                c0 = stcol(8)
                for d, slot in ((0, s0), (1, s1)):
                    b = tm_group(slot, lambda kc, tb: xnT[:, kc, tb * 128:(tb + 1) * 128], [("xnT", tb)], tb)
                    small("dve", lambda e, b=b, d=d: e.tensor_tensor(out=vtmp[d], in0=banks[b][:],
                                                                     in1=cst[:, 1024 + d * 512:1024 + (d + 1) * 512], op=ALU.add),
                          [("ps", b), ("cst",)], [("TT", "vtmp", d)])
                    small("act", lambda e, d=d, vb=vb, c0=c0: e.activation(
                        out=vbuf[vb][:, d * 512:(d + 1) * 512], in_=vtmp[d], func=AF.Gelu, accum_out=st[:, c0 + d:c0 + d + 1]),
                        [("TT", "vtmp", d)], [("TT", "vbuf", vb, d), ("st", c0 + d)])
                vk = [("TT", "vbuf", vb, 0), ("TT", "vbuf", vb, 1)]
                small("act", lambda e, vb=vb, c0=c0, tb=tb: e.activation(out=vhat[:, tb, :], in_=vbuf[vb], func=AF.Square,
                                                                          accum_out=st[:, c0 + 2:c0 + 3]),
                      vk, [("st", c0 + 2), ("R", "vhat", tb)])
                col = lambda k, c0=c0: st[:, c0 + k:c0 + k + 1]
                small("dve", lambda e, col=col: e.tensor_tensor(out=col(3), in0=col(0), in1=col(1), op=ALU.add),
                      [("st", c0), ("st", c0 + 1)], [("st", c0 + 3)])
                small("dve", lambda e, col=col: e.tensor_scalar(out=col(3), in0=col(3), scalar1=1.0 / 1024, scalar2=None,
                                                                op0=ALU.mult), [("st", c0 + 3)], [("st", c0 + 3)])
                small("dve", lambda e, col=col: e.tensor_tensor(out=col(4), in0=col(3), in1=col(3), op=ALU.mult),
                      [("st", c0 + 3)], [("st", c0 + 4)])
                small("dve", lambda e, col=col: e.scalar_tensor_tensor(out=col(4), in0=col(2), scalar=1.0 / 1024, in1=col(4),
                                                                       op0=ALU.mult, op1=ALU.subtract),
                      [("st", c0 + 2), ("st", c0 + 4)], [("st", c0 + 4)])
                small("dve", lambda e, col=col: e.tensor_scalar(out=col(5), in0=col(4), scalar1=LN_EPS, scalar2=None,
                                                                op0=ALU.add), [("st", c0 + 4)], [("st", c0 + 5)])
                small("pool", lambda e, col=col: e.tensor_tensor(out=col(5), in0=col(5), in1=mhalf[:, 0:1], op=ALU.pow),
                      [("st", c0 + 5), ("mhalf",)], [("st", c0 + 5)])
                small("dve", lambda e, col=col: e.scalar_tensor_tensor(out=col(6), in0=col(3), scalar=-1.0, in1=col(5),
                                                                       op0=ALU.mult, op1=ALU.mult),
                      [("st", c0 + 3), ("st", c0 + 5)], [("st", c0 + 6)])
                small("act", lambda e, vb=vb, tb=tb, col=col: e.activation(out=vhat[:, tb, :], in_=vbuf[vb], func=AF.Identity,
                                                                           bias=col(6), scale=col(5)),
                      vk + [("st", c0 + 5), ("st", c0 + 6)], [("R", "vhat", tb)])
            def sp_stage(tb):
                for hb in range(2):
                    b = next_bank()

                    def smm(e, b=b, hb=hb, tb=tb):
                        for j in range(4):
                            g = hb * 4 + j
                            r = e.matmul(banks[b][:, j * 128:(j + 1) * 128], lhsT=vhat[:, tb, g * 128:(g + 1) * 128],
                                         rhs=wsT_bf[:, g, :], start=True, stop=True)
                        return r
                    P.op("pe", smm, reads=[("R", "vhat", tb), ("wsT_bf",)], writes=[("ps", b)])
                    sb_ = (tb % 2) * 2 + hb
                    sv3 = svtmp[sb_].rearrange("p (g t) -> p g t", g=4)
                    if hb == 0:
                        def gain_act(e, b=b, hb=hb, sv3=sv3):
                            for j in range(4):
                                r = e.activation(out=sv3[:, j, :], in_=banks[b][:, j * 128:(j + 1) * 128], func=AF.Copy,
                                                 scale=sm[:, SM_LNGV + hb * 4 + j:SM_LNGV + hb * 4 + j + 1])
                            return r
                        P.op("act", gain_act, reads=[("ps", b), ("sm",)], writes=[("TT", "svtmp", sb_)])
                    else:
                        small("dve", lambda e, b=b, hb=hb, sv3=sv3: e.tensor_tensor(
                            out=sv3, in0=banks[b][:].rearrange("p (g t) -> p g t", g=4),
                            in1=sm[:, SM_LNGV + hb * 4:SM_LNGV + hb * 4 + 4].unsqueeze(2).to_broadcast([128, 4, 128]),
                            op=ALU.mult), [("ps", b), ("sm",)], [("TT", "svtmp", sb_)])
                    small("dve", lambda e, hb=hb, sv3=sv3: e.tensor_tensor(
                        out=sv3, in0=sv3, in1=biasT[:, hb * 4:hb * 4 + 4, :], op=ALU.add),
                        [("TT", "svtmp", sb_), ("biasT", hb)], [("TT", "svtmp", sb_)])
                    small("pool", lambda e, hb=hb, sv3=sv3, tb=tb: e.tensor_tensor(
                        out=yT[:, hb * 4:hb * 4 + 4, tb * 128:(tb + 1) * 128], in0=sv3,
                        in1=uT[:, hb * 4:hb * 4 + 4, tb * 128:(tb + 1) * 128], op=ALU.mult),
                        [("TT", "svtmp", sb_)] + [("R", "uT", hb * 4 + j, tb // 4) for j in range(4)],
                        [("R", "yT", tb, hb)])
            for tb in range(NTB):
                v_stage(tb)
            release(i0)
            release(i1)
            for j in range(2):
                i, slot = take(("od_w_in", 0, j * 512))
                for m in range(4):
                    c = j * 4 + m
                    for s in range(2):
                        b = fm_group(slot, m, s)
                        small("act", lambda e, b=b, c=c, s=s: e.activation(
                            out=uT[:, c, s * 512:(s + 1) * 512], in_=banks[b][:], func=AF.Gelu,
                            bias=sm[:, SM_BU + c:SM_BU + c + 1], scale=1.0), [("ps", b), ("sm",)], [("R", "uT", c, s)])
                release(i)
            for tb in range(NTB):
                sp_stage(tb)
            out_proj("od_w_out", yT, lambda tb: [("R", "yT", tb, 0), ("R", "yT", tb, 1)], make_tail("norm", 3))

        for tb in range(NTB):
            load_x(0, tb)
        for mt in range(n_mt):
            for li, L in enumerate(layers):
                if L == 0:
                    mixer0(mt)
                else:
                    mixer1(mt)
                P.fence("R")
                P.fence("TT")
                if li + 1 < len(layers):
                    tail = make_tail("norm", 2 if layers[li + 1] == 1 else 0)
                elif final_norm:
                    tail = make_tail("final", mt=mt)
                else:
                    tail = make_tail(None)
                hold_final[0] = (li + 1 == len(layers)) and final_norm and (mt + 1 < n_mt)
                mlp(L, tail)
                hold_final[0] = False
                P.fence("R")
                P.fence("TT")
            if not final_norm:
                for tb in range(NTB):
                    dst = y[mt * T + tb * 128:mt * T + (tb + 1) * 128, :]
                    P.op("sp", lambda e, tb=tb, dst=dst: e.dma_start(out=dst, in_=h[:, tb, :]), reads=h_keys(tb), dma="out")
                    if mt + 1 < n_mt:
                        load_x(mt + 1, tb)
            P.fence("TT")
        assert cur[0] == len(pieces)
        cnt = P.emit(nc, final_waits=["out"])
        build_program.info = dict(signals=cnt, nops=len(P.ops), nsem=P.nsem)
    return nc


def _pack_inputs(inp):
    f = lambda a: np.ascontiguousarray(a, dtype=np.float32)
    pc = lambda v, n: np.asarray(v, np.float32).reshape(n, 128).T
    sm = np.zeros((128, SM_N), np.float32)
    gs = [inp["ev_norm_g"][0], inp["mlp_norm_g"][0], inp["od_norm_g"][0], inp["mlp_norm_g"][1]]
    for i, g in enumerate(gs):
        sm[:, SM_G + 8 * i:SM_G + 8 * i + 8] = pc(g, 8)
    caw = np.asarray(inp["ev_conv_a_w"][0], np.float32)
    sm[:, SM_CA:SM_CA + 124] = caw.T.reshape(4, 128, 31).transpose(1, 0, 2).reshape(128, 124)
    sm[:, SM_CAB:SM_CAB + 4] = pc(inp["ev_conv_a_b"][0], 4)
    sm[:, SM_LNG:SM_LNG + 4] = pc(inp["ev_ln_a_g"][0], 4)
    sm[:, SM_LNB:SM_LNB + 4] = pc(inp["ev_ln_a_b"][0], 4)
    cbw = np.asarray(inp["ev_conv_b_w"][0], np.float32)
    sm[:, SM_CB:SM_CB + 12] = cbw.T.reshape(4, 128, 3).transpose(1, 0, 2).reshape(128, 12)
    b_in = np.asarray(inp["od_b_in"][0], np.float32)
    sm[:, SM_BU:SM_BU + 8] = pc(b_in[:1024], 8)
    sm[:, SM_LNGV:SM_LNGV + 8] = pc(inp["od_ln_v_g"][0], 8)
    sm[:, SM_LNBV:SM_LNBV + 8] = pc(inp["od_ln_v_b"][0], 8)
    big = np.zeros((128, 4096), np.float32)
    big[:, 0:1024] = np.asarray(inp["final_norm_g"], np.float32)[None, :]
    big[:, 1024:2048] = b_in[None, 1024:]
    ws = np.asarray(inp["od_w_s"][0], np.float32)
    big[:, 2048:3072] = ws.transpose(2, 0, 1).reshape(128, 1024)
    big[:, 3072:4096] = np.asarray(inp["od_b_s"][0], np.float32).reshape(1, 1024)
    com = {
        "ev_w_in": f(inp["ev_w_in"][0]), "ev_w_out": f(inp["ev_w_out"][0]),
        "od_w_in": f(inp["od_w_in"][0]), "od_w_out": f(inp["od_w_out"][0]),
        "w1_0": f(inp["mlp_w1"][0]), "w1_1": f(inp["mlp_w1"][1]),
        "w2_0": f(inp["mlp_w2"][0]), "w2_1": f(inp["mlp_w2"][1]),
        "smallp": sm, "bigc": big,
    }
    return com


_CACHE = {}


def _prog(layers, final_norm):
    k = (tuple(layers), final_norm)
    if k not in _CACHE:
        _CACHE[k] = build_program(layers, final_norm)
    return _CACHE[k]


FUSED = True


def kernel(**inputs):
    com = _pack_inputs(inputs)
    x = np.ascontiguousarray(np.asarray(inputs["x"], np.float32)).reshape(NCORES, TOK, 1024)
    cores = list(range(NCORES))
    stages = [((0, 1), True)] if FUSED else [((0,), False), ((1,), True)]
    cur = x
    for layers, fin in stages:
        nc = _prog(layers, fin)
        in_maps = [dict(com, x=np.ascontiguousarray(cur[c])) for c in cores]
        res = run_bass_kernel_spmd(nc, in_maps, core_ids=cores)
        cur = np.stack([res.results[c]["y"] for c in cores], axis=0)
    return cur.reshape(16, 2048, 1024).astype(np.float32)
```

```python
import contextlib
import numpy as np
import concourse.bass as bass
import concourse.mybir as mybir
from concourse.bass_utils import run_bass_kernel_spmd

F32 = mybir.dt.float32
BF16 = mybir.dt.bfloat16
AF = mybir.ActivationFunctionType
ALU = mybir.AluOpType

ENGS = ("pe", "act", "dve", "pool", "sp")
LIM = 240
DLIM = 15

NCORES = 8
TOK = 4096
T = 1024
NTB = 8
NSLOT = 4
NDG = 8
RMS_EPS = 1e-6
LN_EPS = 1e-5


class _Op:
    __slots__ = ("eng", "fn", "reads", "writes", "dma", "src", "pos", "waits", "signal", "sigval")

    def __init__(self, eng, fn, reads, writes, dma):
        self.eng, self.fn, self.reads, self.writes, self.dma = eng, fn, reads, writes, dma
        self.waits = []
        self.signal = False
        self.sigval = 0


class Prog:
    def __init__(self):
        self.ops = []
        self.state = {}
        self.region_init = {}
        self.eng_count = {e: 0 for e in ENGS}
        self.dma_count = {}
        self.waited = {e: {} for e in ENGS}
        self.by_src = {}

    def _st(self, k):
        st = self.state.get(k)
        if st is None:
            st = [None, dict(self.region_init.get(k[0], {}))]
            self.state[k] = st
        return st

    def fence(self, region):
        merged = dict(self.region_init.get(region, {}))
        for k in [k for k in self.state if k[0] == region]:
            w, rd = self.state.pop(k)
            if w is not None:
                merged[w[0]] = max(merged.get(w[0], 0), w[1])
            for s, p in rd.items():
                merged[s] = max(merged.get(s, 0), p)
        self.region_init[region] = merged

    def op(self, eng, fn, reads=(), writes=(), dma=None):
        o = _Op(eng, fn, tuple(reads), tuple(writes), dma)
        deps = {}

        def need(src, pos):
            if deps.get(src, 0) < pos:
                deps[src] = pos

        for k in o.reads:
            st = self._st(k)
            if st[0] is not None:
                need(*st[0])
        for k in o.writes:
            st = self._st(k)
            if st[0] is not None:
                need(*st[0])
            for s, p in st[1].items():
                need(s, p)
        snap = dict(self.dma_count)
        if dma is not None:
            src = ("dma", dma)
            self.dma_count[dma] = self.dma_count.get(dma, 0) + 1
            pos = self.dma_count[dma]
        else:
            src = eng
            self.eng_count[eng] += 1
            pos = self.eng_count[eng]
        o.src, o.pos = src, pos
        self.by_src[(src, pos)] = o
        for k in o.reads:
            st = self._st(k)
            if st[1].get(src, 0) < pos:
                st[1][src] = pos
        for k in o.writes:
            st = self._st(k)
            st[0] = (src, pos)
            st[1] = {}
        wt = self.waited[eng]
        for s, p in deps.items():
            if s == "pe" and eng == "pe" and dma is None:
                continue
            if isinstance(s, tuple):
                p = snap[s[1]]
            prev = wt.get(s, 0)
            if prev >= p:
                continue
            wt[s] = p
            o.waits.append((s, p, prev))
            if not isinstance(s, tuple):
                self.by_src[(s, p)].signal = True
        self.ops.append(o)
        return o

    def emit(self, nc, final_waits=()):
        cnt = {e: 0 for e in ENGS}
        for o in self.ops:
            if o.dma is None and o.signal:
                cnt[o.eng] += 1
                o.sigval = cnt[o.eng]
        with contextlib.ExitStack() as es:
            sems = {}
            nsem = 0
            for e in ENGS:
                for ep in range((cnt[e] + LIM - 1) // LIM):
                    sems[(e, ep)] = es.enter_context(nc.semaphore(f"s_{e}_{ep}"))
                    nsem += 1
            for d, n in self.dma_count.items():
                for ep in range((n + DLIM - 1) // DLIM):
                    sems[(("dma", d), ep)] = es.enter_context(nc.semaphore(f"d_{d}_{ep}"))
                    nsem += 1
            self.nsem = nsem
            block = es.enter_context(nc.Block())
            regs = {"pe": block.tensor, "act": block.scalar, "dve": block.vector,
                    "pool": block.gpsimd, "sp": block.sync}

            def dma_wait(engine, s, p, prev):
                for ep in range(prev // DLIM, (p - 1) // DLIM + 1):
                    hi = min(p, (ep + 1) * DLIM) - ep * DLIM
                    engine.wait_ge(sems[(s, ep)], 16 * hi)

            for e in ENGS:
                my = [o for o in self.ops if o.eng == e]
                if not my and not (e == "sp" and final_waits):
                    continue

                def body(engine, my=my, e=e):
                    for o in my:
                        for s, p, prev in o.waits:
                            if isinstance(s, tuple):
                                dma_wait(engine, s, p, prev)
                            else:
                                c = self.by_src[(s, p)].sigval
                                engine.wait_ge(sems[(s, (c - 1) // LIM)], (c - 1) % LIM + 1)
                        ins = o.fn(engine)
                        if o.dma is not None:
                            ins.then_inc(sems[(o.src, (o.pos - 1) // DLIM)], 16)
                        elif o.signal:
                            ins.then_inc(sems[(e, (o.sigval - 1) // LIM)], 1)
                    if e == "sp":
                        for d in final_waits:
                            dma_wait(engine, ("dma", d), self.dma_count[d], 0)

                regs[e](body)
        return cnt


SM_G = 0
SM_CA = 32
SM_CAB = 156
SM_LNG = 160
SM_LNB = 164
SM_CB = 168
SM_BU = 180
SM_LNGV = 188
SM_LNBV = 196
SM_N = 204


def build_program(layers=(0, 1), final_norm=True, n_mt=4):
    nc = bass.Bass("TRN2", target_bir_lowering=False)
    dram = lambda n, s, k="ExternalInput": nc.dram_tensor(n, s, F32, kind=k).ap()
    x = dram("x", [TOK, 1024])
    y = dram("y", [TOK, 1024], "ExternalOutput")
    W = {
        "ev_w_in": dram("ev_w_in", [1024, 2560]), "ev_w_out": dram("ev_w_out", [1024, 1024]),
        "od_w_in": dram("od_w_in", [1024, 2048]), "od_w_out": dram("od_w_out", [1024, 1024]),
        "w1_0": dram("w1_0", [1024, 4096]), "w1_1": dram("w1_1", [1024, 4096]),
        "w2_0": dram("w2_0", [4096, 1024]), "w2_1": dram("w2_1", [4096, 1024]),
    }
    smallp = dram("smallp", [128, SM_N])
    bigc = dram("bigc", [128, 4096])

    P = Prog()
    with contextlib.ExitStack() as es:
        sb = lambda name, shape, dt=F32: es.enter_context(nc.sbuf_tensor(name, shape, dt))
        h = sb("h", [128, NTB, 1024])
        xnT = sb("xnT", [128, 8, T], BF16)
        R = sb("R", [128, 12288])
        TT = sb("TT", [128, 12032])
        ring = [sb(f"ring{i}", [128, 8, 512], BF16) for i in range(NSLOT)]
        xn_tm = [sb(f"xn_tm{i}", [128, 1024], BF16) for i in range(3)]
        junk = [sb(f"junk{i}", [128, 1024], BF16) for i in range(2)]
        identf = sb("identf", [128, 128])
        dg = [sb(f"dg{i}", [128, 128], BF16) for i in range(NDG)]
        sm = sb("sm", [128, SM_N])
        cst = sb("cst", [128, 2048])
        biasT = sb("biasT", [128, 8, 128])
        wsT_bf = sb("wsT_bf", [128, 8, 128], BF16)
        ident = sb("ident", [128, 128], BF16)
        ones_f = sb("ones_f", [128, 128])
        mhalf = sb("mhalf", [128, 512])
        st = sb("st", [128, 64])
        halo_a = sb("halo_a", [128, 4, 30], BF16)
        halo_c = sb("halo_c", [128, 4, 2])
        banks = [es.enter_context(nc.psum_tensor(f"bank{i}", [128, 512], F32)) for i in range(8)]

        def rview(base, off, n, dt=F32):
            nb = n * (2 if dt == BF16 else 4)
            assert off % 4 == 0 and nb % 4 == 0
            ap = base[:, off // 4:(off + nb) // 4]
            return ap.bitcast(BF16) if dt == BF16 else ap

        KB = 1024
        a_conv = rview(R, 0, 4 * T).rearrange("p (m t) -> p m t", m=4)
        cconv = rview(R, 16 * KB, 4 * T).rearrange("p (m t) -> p m t", m=4)
        mixT = rview(R, 32 * KB, 8 * T, BF16).rearrange("p (m t) -> p m t", m=8)
        hid = [rview(R, i * 16 * KB, 8 * T, BF16).rearrange("p (m t) -> p m t", m=8) for i in range(2)]
        uT = rview(R, 0, 8 * T, BF16).rearrange("p (m t) -> p m t", m=8)
        vhat = rview(R, 16 * KB, 8 * 1024, BF16).rearrange("p (b c) -> p b c", b=8)
        yT = rview(R, 32 * KB, 8 * T, BF16).rearrange("p (m t) -> p m t", m=8)
        o = 0

        def talloc(n, dt=F32):
            nonlocal o
            v = rview(TT, o, n, dt)
            o += n * (2 if dt == BF16 else 4)
            assert o <= 12032 * 4, o
            return v
        abuf = [talloc(32 + T, BF16) for _ in range(2)]
        cgbuf = [talloc(2 + T) for _ in range(2)]
        ccB = talloc(T)
        sig = [talloc(512) for _ in range(2)]
        ln_mean = [talloc(512) for _ in range(2)]
        ln_t = [talloc(512) for _ in range(2)]
        ln_rstd = talloc(512)
        dbuf = [talloc(512) for _ in range(2)]
        sqb = [talloc(512) for _ in range(2)]
        o = 0
        rbuf = [talloc(512) for _ in range(3)]
        o = 0
        vtmp = [talloc(512) for _ in range(2)]
        vbuf = [talloc(1024) for _ in range(2)]
        svtmp = [talloc(512) for _ in range(4)]
        o = 8 * KB
        obuf = [talloc(1024) for _ in range(8)]
        o = 0
        wsT_f = talloc(1024).rearrange("p (g t) -> p g t", g=8)
        bs_bc = talloc(1024).rearrange("p (g t) -> p g t", g=8)
        tmpB = talloc(512).rearrange("p (g t) -> p g t", g=4)

        bank_ctr = [0]

        def next_bank():
            b = bank_ctr[0] % 8
            bank_ctr[0] += 1
            return b

        pieces = []

        def plan_pieces():
            for mt in range(n_mt):
                for L in layers:
                    if L == 0:
                        for c in (1, 0, 3, 4, 2):
                            pieces.append(("ev_w_in", 0, c * 512))
                        for d in range(2):
                            pieces.append(("ev_w_out", 0, d * 512))
                    else:
                        for c in (2, 3, 0, 1):
                            pieces.append(("od_w_in", 0, c * 512))
                        for d in range(2):
                            pieces.append(("od_w_out", 0, d * 512))
                    for fq in range(4):
                        pieces.append((f"w1_{L}", 0, (2 * fq) * 512))
                        pieces.append((f"w1_{L}", 0, (2 * fq + 1) * 512))
                        pieces.append((f"w2_{L}", fq * 1024, 0))
                        pieces.append((f"w2_{L}", fq * 1024, 512))
        plan_pieces()
        cur = [0]
        issued = [0]

        def issue(i):
            if i >= len(pieces):
                return
            assert i == issued[0]
            issued[0] += 1
            wn, r0, c0 = pieces[i]
            s = i % NSLOT
            src = W[wn][r0:r0 + 1024, c0:c0 + 512].rearrange("(kc p) n -> p kc n", p=128)
            P.op("pool", lambda e, s=s, src=src: e.dma_start(out=ring[s][:], in_=src),
                 writes=[("w", s)], dma=f"w{s}")

        def take(expect):
            i = cur[0]
            assert pieces[i] == expect, (pieces[i], expect)
            assert i < issued[0], (i, issued[0])
            cur[0] += 1
            return i, i % NSLOT

        def release(i):
            issue(i + NSLOT)

        P.op("sp", lambda e: e.dma_start(out=sm[:], in_=smallp), writes=[("sm",)], dma="c")
        P.op("sp", lambda e: e.dma_start(out=cst[:], in_=bigc[:, 0:2048]), writes=[("cst",)], dma="c")
        for i in range(2):
            issue(i)
        late_issue = list(range(2, NSLOT))
        P.op("pool", lambda e: e.memset(identf[:], 0.0), writes=[("identf",)])
        P.op("pool", lambda e: e.affine_select(out=identf[:], in_=identf[:], pattern=[[-1, 128]],
                                                compare_op=ALU.not_equal, fill=1.0, base=0,
                                                channel_multiplier=1),
             reads=[("identf",)], writes=[("identf",)])
        P.op("pool", lambda e: e.tensor_copy(out=ident[:], in_=identf[:]), reads=[("identf",)], writes=[("ident",)])
        P.op("pool", lambda e: e.memset(ones_f[:], 1.0), writes=[("ones",)])
        P.op("pool", lambda e: e.memset(mhalf[:], -0.5), writes=[("mhalf",)])
        P.op("pool", lambda e: e.memset(halo_a[:], 0.0), writes=[("halo_a", m) for m in range(4)])
        P.op("pool", lambda e: e.memset(halo_c[:], 0.0), writes=[("halo_c", m) for m in range(4)])
        if 1 in layers:
            P.op("sp", lambda e: e.dma_start(out=wsT_f, in_=bigc[:, 2048:3072].rearrange("p (g t) -> p g t", g=8)),
                 writes=[("TT", "wsT_f")], dma="c")
            P.op("sp", lambda e: e.dma_start(out=bs_bc, in_=bigc[:, 3072:4096].rearrange("p (g t) -> p g t", g=8)),
                 writes=[("TT", "bs_bc")], dma="c")
            P.op("pool", lambda e: e.affine_select(out=wsT_f, in_=wsT_f, pattern=[[0, 8], [1, 128]],
                                                    compare_op=ALU.is_ge, fill=0.0, base=0,
                                                    channel_multiplier=-1),
                 reads=[("TT", "wsT_f")], writes=[("TT", "wsT_f")])
            P.op("pool", lambda e: e.tensor_copy(out=wsT_bf[:], in_=wsT_f), reads=[("TT", "wsT_f")], writes=[("wsT_bf",)])
            for hb in range(2):
                b = next_bank()

                def rs_mm(e, hb=hb, b=b):
                    for j in range(4):
                        r = e.matmul(banks[b][:, j * 128:(j + 1) * 128], lhsT=ones_f[:], rhs=wsT_f[:, hb * 4 + j, :],
                                     start=True, stop=True)
                    return r
                P.op("pe", rs_mm, reads=[("TT", "wsT_f"), ("ones",)], writes=[("ps", b)])
                P.op("dve", lambda e, hb=hb, b=b: e.tensor_tensor(
                    out=tmpB, in0=banks[b][:].rearrange("p (g t) -> p g t", g=4),
                    in1=sm[:, SM_LNBV + hb * 4:SM_LNBV + hb * 4 + 4].unsqueeze(2).to_broadcast([128, 4, 128]),
                    op=ALU.mult), reads=[("ps", b), ("sm",)], writes=[("TT", "tmpB")])
                P.op("dve", lambda e, hb=hb: e.tensor_tensor(
                    out=biasT[:, hb * 4:hb * 4 + 4, :], in0=tmpB, in1=bs_bc[:, hb * 4:hb * 4 + 4, :], op=ALU.add),
                    reads=[("TT", "tmpB"), ("TT", "bs_bc")], writes=[("biasT", hb)])
        P.fence("TT")

        def small(eng, fn, reads, writes):
            P.op(eng, fn, reads=reads, writes=writes)

        stc = [0]

        def stcol(n=1):
            c = stc[0]
            if c + n > 64:
                c = 0
            stc[0] = (c + n) % 64
            return c

        def rstd_from_ss(ss_col, out_col, inv_n, eps, rkeys):
            small("dve", lambda e: e.tensor_scalar(out=st[:, out_col:out_col + 1], in0=st[:, ss_col:ss_col + 1],
                                                   scalar1=inv_n, scalar2=eps, op0=ALU.mult, op1=ALU.add),
                  rkeys, [("st", out_col)])
            small("pool", lambda e: e.tensor_tensor(out=st[:, out_col:out_col + 1], in0=st[:, out_col:out_col + 1],
                                                    in1=mhalf[:, 0:1], op=ALU.pow),
                  [("st", out_col), ("mhalf",)], [("st", out_col)])

        def h_keys(tb):
            return [("h", tb, 0), ("h", tb, 1)]

        def batched_rstd(c0, inv_n, eps):
            small("dve", lambda e: e.tensor_scalar(out=st[:, c0 + 8:c0 + 16], in0=st[:, c0:c0 + 8],
                                                   scalar1=inv_n, scalar2=eps, op0=ALU.mult, op1=ALU.add),
                  [("st", c0 + i) for i in range(8)], [("st", c0 + 8 + i) for i in range(8)])
            small("act", lambda e: e.activation(out=st[:, c0 + 8:c0 + 16], in_=st[:, c0 + 8:c0 + 16], func=AF.Sqrt),
                  [("st", c0 + 8 + i) for i in range(8)], [("st", c0 + 8 + i) for i in range(8)])
            small("dve", lambda e: e.reciprocal(out=st[:, c0 + 8:c0 + 16], in_=st[:, c0 + 8:c0 + 16]),
                  [("st", c0 + 8 + i) for i in range(8)], [("st", c0 + 8 + i) for i in range(8)])

        def stat_tb(c0, tb):
            jb = tb % 2
            small("act", lambda e: e.activation(out=junk[jb][:], in_=h[:, tb, :], func=AF.Square,
                                                accum_out=st[:, c0 + tb:c0 + tb + 1]),
                  h_keys(tb), [("st", c0 + tb), ("junk", jb)])

        pre = [None]

        def tail_hook(tb):
            if pre[0] is not None:
                stat_tb(pre[0], tb)

        def norm_stats():
            if pre[0] is None:
                c0 = stcol(16)
                for tb in range(NTB):
                    stat_tb(c0, tb)
            else:
                c0 = pre[0]
                pre[0] = None
            batched_rstd(c0, 1.0 / 1024, RMS_EPS)
            return c0

        LAG = 2
        xn_done = [False]
        early = [None]
        hold_release = [False]
        hold_final = [False]
        held = []
        deferred_tr = []
        fm_since = [0]

        def flush_deferred():
            while deferred_tr:
                nidx, tb = deferred_tr.pop(0)
                tr_op(nidx, tb)

        def xn_op(c0, tb, eng="act"):
            xb = tb % 3
            if eng == "act":
                small("act", lambda e: e.activation(out=xn_tm[xb][:], in_=h[:, tb, :], func=AF.Copy,
                                                    scale=st[:, c0 + 8 + tb:c0 + 9 + tb]),
                      h_keys(tb) + [("st", c0 + 8 + tb)], [("xn_tm", xb)])
            else:
                small("dve", lambda e: e.tensor_scalar(out=xn_tm[xb][:], in0=h[:, tb, :],
                                                       scalar1=st[:, c0 + 8 + tb:c0 + 9 + tb], scalar2=None, op0=ALU.mult),
                      h_keys(tb) + [("st", c0 + 8 + tb)], [("xn_tm", xb)])

        def tr_op(nidx, tb):
            xb = tb % 3
            b = next_bank()
            tbk = banks[b][:].bitcast(BF16)

            def tr(e):
                for kc in range(8):
                    r = e.transpose(tbk[:, kc * 128:(kc + 1) * 128], xn_tm[xb][:, kc * 128:(kc + 1) * 128], ident[:])
                return r
            P.op("pe", tr, reads=[("xn_tm", xb), ("ident",)], writes=[("ps", b)])
            small("dve", lambda e: e.tensor_tensor(
                out=xnT[:, :, tb * 128:(tb + 1) * 128], in0=tbk.rearrange("p (k n) -> p k n", k=8),
                in1=sm[:, SM_G + nidx * 8:SM_G + nidx * 8 + 8].unsqueeze(2).to_broadcast([128, 8, 128]),
                op=ALU.mult), [("ps", b), ("sm",)], [("xnT", tb)])

        def rstd_tb(c0, tb):
            col = st[:, c0 + 8 + tb:c0 + 9 + tb]
            k = [("st", c0 + 8 + tb)]
            small("dve", lambda e: e.tensor_scalar(out=col, in0=st[:, c0 + tb:c0 + tb + 1], scalar1=1.0 / 1024,
                                                   scalar2=RMS_EPS, op0=ALU.mult, op1=ALU.add), [("st", c0 + tb)], k)
            small("act", lambda e: e.activation(out=col, in_=col, func=AF.Sqrt), k, k)
            small("dve", lambda e: e.reciprocal(out=col, in_=col), k, k)

        def norm_to_xnT(nidx):
            if xn_done[0]:
                xn_done[0] = False
                return
            pre_c0 = early[0]
            early[0] = None
            c0 = stcol(16) if pre_c0 is None else pre_c0
            if pre_c0 is None:
                for tb in range(3):
                    stat_tb(c0, tb)
            LATE = 6
            for tb in range(NTB):
                if tb >= LAG:
                    tr_op(nidx, tb - LAG)
                if tb >= LATE:
                    stat_tb(c0, tb)
                if pre_c0 is not None and tb == 0:
                    rstd_rcp(c0, 0)
                else:
                    rstd_tb(c0, tb)
                if tb + 3 < LATE:
                    stat_tb(c0, tb + 3)
                xn_op(c0, tb, "act" if tb % 2 == 0 else "dve")
            for tb in range(NTB - LAG, NTB):
                tr_op(nidx, tb)

        def rstd_aff(c0, tb):
            col = st[:, c0 + 8 + tb:c0 + 9 + tb]
            small("dve", lambda e: e.tensor_scalar(out=col, in0=st[:, c0 + tb:c0 + tb + 1], scalar1=1.0 / 1024,
                                                   scalar2=RMS_EPS, op0=ALU.mult, op1=ALU.add),
                  [("st", c0 + tb)], [("st", c0 + 8 + tb)])

        def rstd_sqrt(c0, tb):
            col = st[:, c0 + 8 + tb:c0 + 9 + tb]
            k = [("st", c0 + 8 + tb)]
            small("act", lambda e: e.activation(out=col, in_=col, func=AF.Sqrt), k, k)

        def rstd_rcp(c0, tb):
            col = st[:, c0 + 8 + tb:c0 + 9 + tb]
            k = [("st", c0 + 8 + tb)]
            small("dve", lambda e: e.reciprocal(out=col, in_=col), k, k)

        def make_tail(kind, nidx=None, mt=None):
            if kind is None:
                return (lambda tb: None), (lambda: None)
            c0 = stcol(16)
            nc0 = None
            if kind == "final" and mt + 1 < n_mt:
                nc0 = stcol(16)
                early[0] = nc0

            def consume(tb):
                rstd_rcp(c0, tb)
                if kind == "norm":
                    xn_op(c0, tb)
                else:
                    final_tb(c0, tb, mt)

            def post(tb):
                if tb >= 2:
                    consume(tb - 2)
                if tb >= 1:
                    rstd_aff(c0, tb - 1)
                    rstd_sqrt(c0, tb - 1)
                stat_tb(c0, tb)
                if kind == "norm" and tb >= 3:
                    tr_op(nidx, tb - 3)
                if nc0 is not None:
                    if tb == 6:
                        stat_tb(nc0, 0)
                        stat_tb(nc0, 1)
                    elif tb == 7:
                        stat_tb(nc0, 2)
                        rstd_aff(nc0, 0)
                        rstd_sqrt(nc0, 0)

            def flush():
                consume(NTB - 2)
                rstd_aff(c0, NTB - 1)
                rstd_sqrt(c0, NTB - 1)
                if kind == "norm":
                    tr_op(nidx, NTB - 3)
                consume(NTB - 1)
                if kind == "norm":
                    for tb in range(NTB - 2, NTB):
                        deferred_tr.append((nidx, tb))
                    fm_since[0] = 0
                    xn_done[0] = True
            return post, flush

        def final_tb(c0, tb, mt):
            dst = y[mt * T + tb * 128:mt * T + (tb + 1) * 128, :]
            small("dve", lambda e: e.scalar_tensor_tensor(
                out=obuf[tb], in0=h[:, tb, :], scalar=st[:, c0 + 8 + tb:c0 + 9 + tb], in1=cst[:, 0:1024],
                op0=ALU.mult, op1=ALU.mult), h_keys(tb) + [("st", c0 + 8 + tb), ("cst",)], [("TT", "obuf", tb)])
            P.op("sp", lambda e: e.dma_start(out=dst, in_=obuf[tb]), reads=[("TT", "obuf", tb)], dma="out")
            if mt + 1 < n_mt:
                load_x(mt + 1, tb)

        def load_x(mt, tb):
            r0 = mt * T
            P.op("sp", lambda e: e.dma_start(out=h[:, tb, :], in_=x[r0 + tb * 128:r0 + (tb + 1) * 128, :]),
                 writes=h_keys(tb), dma=f"x{tb}")

        def tm_pair(pieces2, lhs, keys_fn, tail):
            post, flush = tail
            for tb in range(NTB):
                for d, (i, slot) in enumerate(pieces2):
                    b = tm_group(slot, lhs, keys_fn(tb), tb)
                    add_to_h(b, tb, d)
                post(tb)
            flush()
            for i, slot in pieces2:
                if hold_release[0]:
                    held.append(i)
                else:
                    release(i)
            hold_release[0] = False

        def fm_group(slot, m, s):
            if deferred_tr and (s == 1 or fm_since[0] >= 2):
                flush_deferred()
            fm_since[0] += 1
            b = next_bank()

            def mm(e):
                for kc in range(8):
                    r = e.matmul(banks[b][:], lhsT=ring[slot][:, kc, m * 128:(m + 1) * 128],
                                 rhs=xnT[:, kc, s * 512:(s + 1) * 512], start=(kc == 0), stop=(kc == 7))
                return r
            P.op("pe", mm, reads=[("w", slot)] + [("xnT", tb) for tb in range(4 * s, 4 * s + 4)], writes=[("ps", b)])
            return b

        def tm_group(slot, lhs, lhs_keys, tb):
            b = next_bank()

            def mm(e):
                for kc in range(8):
                    r = e.matmul(banks[b][:], lhsT=lhs(kc, tb), rhs=ring[slot][:, kc, :],
                                 start=(kc == 0), stop=(kc == 7))
                return r
            P.op("pe", mm, reads=[("w", slot)] + lhs_keys, writes=[("ps", b)])
            return b

        def add_to_h(b, tb, d):
            small("dve", lambda e: e.tensor_tensor(out=h[:, tb, d * 512:(d + 1) * 512], in0=banks[b][:],
                                                   in1=h[:, tb, d * 512:(d + 1) * 512], op=ALU.add),
                  [("ps", b), ("h", tb, d)], [("h", tb, d)])

        def out_proj(wname, srcT, key_fn, tail):
            p2 = [take((wname, 0, 0)), take((wname, 0, 512))]
            tm_pair(p2, lambda kc, tb: srcT[:, kc, tb * 128:(tb + 1) * 128], key_fn, tail)

        def mlp(L, tail):
            norm_to_xnT(1 if L == 0 else 3)
            for fq in range(4):
                hq = hid[fq % 2]
                for j in range(2):
                    i, slot = take((f"w1_{L}", 0, (2 * fq + j) * 512))
                    for s in range(2):
                        for m in range(4):
                            b = fm_group(slot, m, s)
                            rb = next_r[0] % 3
                            next_r[0] += 1
                            small("act", lambda e, b=b, rb=rb: e.activation(out=rbuf[rb], in_=banks[b][:], func=AF.Relu),
                                  [("ps", b)], [("TT", "rbuf", rb)])
                            small("pool", lambda e, rb=rb, hq=hq, c=j * 4 + m, s=s: e.tensor_tensor(
                                out=hq[:, c, s * 512:(s + 1) * 512], in0=rbuf[rb], in1=rbuf[rb], op=ALU.mult),
                                [("TT", "rbuf", rb)], [("R", "hid", fq % 2, j * 4 + m, s)])
                    release(i)
                p2 = [take((f"w2_{L}", fq * 1024, 0)), take((f"w2_{L}", fq * 1024, 512))]
                hold_release[0] = hold_final[0] and fq == 3
                tm_pair(p2, lambda kc, tb, hq=hq: hq[:, kc, tb * 128:(tb + 1) * 128],
                        lambda tb, fq=fq: [("R", "hid", fq % 2, c, tb // 4) for c in range(8)],
                        tail if fq == 3 else make_tail(None))
        next_r = [0]
        want_tail = [True]
        dgc = [0]

        def mixer0(mt):
            half = mt % 2
            norm_to_xnT(0)
            while late_issue:
                issue(late_issue.pop(0))
            while held:
                release(held.pop(0))
            ig, sg = take(("ev_w_in", 0, 512))
            iv, sv = take(("ev_w_in", 0, 0))
            def a_glu(m):
                ab = abuf[m % 2]
                abk = ("TT", "abuf", m % 2)
                if half == 0:
                    small("pool", lambda e, ab=ab: e.memset(ab[:, 0:30], 0.0), [], [abk + ("halo",)])
                else:
                    small("pool", lambda e, ab=ab, m=m: e.tensor_copy(out=ab[:, 0:30], in_=halo_a[:, m, :]),
                          [("halo_a", m)], [abk + ("halo",)])
                for s in range(2):
                    bg = fm_group(sg, m, s)
                    bv = fm_group(sv, m, s)
                    sgb = (2 * m + s) % 2
                    small("act", lambda e, bg=bg, sgb=sgb: e.activation(out=sig[sgb], in_=banks[bg][:], func=AF.Sigmoid),
                          [("ps", bg)], [("TT", "sig", sgb)])
                    small("dve", lambda e, bv=bv, sgb=sgb, ab=ab, s=s: e.tensor_tensor(
                        out=ab[:, 30 + s * 512:30 + (s + 1) * 512], in0=banks[bv][:], in1=sig[sgb], op=ALU.mult),
                        [("ps", bv), ("TT", "sig", sgb)], [abk + (s,)])
                allab = [abk + ("halo",), abk + (0,), abk + (1,)]
                small("pool", lambda e, ab=ab, m=m: e.tensor_copy(out=halo_a[:, m, :], in_=ab[:, T:T + 30]),
                      [abk + (1,)], [("halo_a", m)])

            def a_conv31(m):
                ab = abuf[m % 2]
                abk = ("TT", "abuf", m % 2)
                allab = [abk + ("halo",), abk + (0,), abk + (1,)]
                cb0, cb1 = next_bank(), next_bank()
                for k in range(31):
                    j = dgc[0] % NDG
                    dgc[0] += 1
                    small("dve", lambda e, j=j, m=m, k=k: e.tensor_scalar(
                        out=dg[j][:], in0=identf[:], scalar1=sm[:, SM_CA + m * 31 + k:SM_CA + m * 31 + k + 1],
                        scalar2=None, op0=ALU.mult), [("identf",), ("sm",)], [("dg", j)])

                    def tapmm(e, j=j, k=k, ab=ab, cb0=cb0, cb1=cb1):
                        e.matmul(banks[cb0][:], lhsT=dg[j][:], rhs=ab[:, k:k + 512], start=(k == 0), stop=(k == 30))
                        return e.matmul(banks[cb1][:], lhsT=dg[j][:], rhs=ab[:, 512 + k:1024 + k],
                                        start=(k == 0), stop=(k == 30))
                    P.op("pe", tapmm, reads=[("dg", j)] + allab, writes=[("ps", cb0), ("ps", cb1)])
                for s_, cb in ((0, cb0), (1, cb1)):
                    small("act", lambda e, m=m, s_=s_, cb=cb: e.activation(
                        out=a_conv[:, m, s_ * 512:(s_ + 1) * 512], in_=banks[cb][:], func=AF.Identity,
                        bias=sm[:, SM_CAB + m:SM_CAB + m + 1], scale=1.0), [("ps", cb), ("sm",)], [("R", "a_conv", m)])
            a_glu(0)
            for m in range(4):
                if m + 1 < 4:
                    a_glu(m + 1)
                a_conv31(m)
            release(ig)
            release(iv)
            for s in range(2):
                cs = slice(s * 512, (s + 1) * 512)
                b1 = next_bank()

                def mm1(e, b1=b1, cs=cs):
                    for m in range(4):
                        r = e.matmul(banks[b1][:], lhsT=ones_f[:], rhs=a_conv[:, m, cs], start=(m == 0), stop=(m == 3))
                    return r
                P.op("pe", mm1, reads=[("R", "a_conv", m) for m in range(4)] + [("ones",)], writes=[("ps", b1)])
                b2 = next_bank()
                for m in range(4):
                    small("act", lambda e, m=m, cs=cs: e.activation(out=sqb[m % 2], in_=a_conv[:, m, cs], func=AF.Square),
                          [("R", "a_conv", m)], [("TT", "sqb", m % 2)])
                    P.op("pe", lambda e, m=m, b2=b2: e.matmul(banks[b2][:], lhsT=ones_f[:], rhs=sqb[m % 2],
                                                              start=(m == 0), stop=(m == 3)),
                         reads=[("TT", "sqb", m % 2), ("ones",)], writes=[("ps", b2)])
                small("dve", lambda e, b1=b1, s=s: e.tensor_scalar(out=ln_mean[s], in0=banks[b1][:], scalar1=1.0 / 512,
                                                                   scalar2=None, op0=ALU.mult), [("ps", b1)], [("TT", "ln_mean", s)])
                small("dve", lambda e, b2=b2, s=s: e.tensor_scalar(out=ln_t[s], in0=banks[b2][:], scalar1=1.0 / 512,
                                                                   scalar2=None, op0=ALU.mult), [("ps", b2)], [("TT", "ln_t", s)])
            ln_thunks = []

            def lazy(eng, fn, reads, writes):
                ln_thunks.append(lambda: small(eng, fn, reads, writes))

            def drain(k):
                for _ in range(min(k, len(ln_thunks))):
                    ln_thunks.pop(0)()
            for s in range(2):
                cs = slice(s * 512, (s + 1) * 512)
                lazy("dve", lambda e, s=s: e.tensor_tensor(out=ln_rstd, in0=ln_mean[s], in1=ln_mean[s], op=ALU.mult),
                      [("TT", "ln_mean", s)], [("TT", "ln_rstd")])
                lazy("dve", lambda e, s=s: e.scalar_tensor_tensor(out=ln_t[s], in0=ln_t[s], scalar=LN_EPS, in1=ln_rstd,
                                                                  op0=ALU.add, op1=ALU.subtract),
                      [("TT", "ln_t", s), ("TT", "ln_rstd")], [("TT", "ln_t", s)])
                lazy("act", lambda e, s=s: e.activation(out=ln_rstd, in_=ln_t[s], func=AF.Sqrt),
                      [("TT", "ln_t", s)], [("TT", "ln_rstd")])
                lazy("dve", lambda e: e.reciprocal(out=ln_rstd, in_=ln_rstd), [("TT", "ln_rstd")], [("TT", "ln_rstd")])
                for m in range(4):
                    db = m % 2
                    lazy("dve", lambda e, m=m, cs=cs, db=db, s=s: e.tensor_tensor(out=dbuf[db], in0=a_conv[:, m, cs],
                                                                                   in1=ln_mean[s], op=ALU.subtract),
                          [("R", "a_conv", m), ("TT", "ln_mean", s)], [("TT", "dbuf", db)])
                    lazy("dve", lambda e, db=db: e.tensor_tensor(out=dbuf[db], in0=dbuf[db], in1=ln_rstd, op=ALU.mult),
                          [("TT", "dbuf", db), ("TT", "ln_rstd")], [("TT", "dbuf", db)])
                    lazy("act", lambda e, m=m, cs=cs, db=db: e.activation(
                        out=mixT[:, m, cs], in_=dbuf[db], func=AF.Silu, bias=sm[:, SM_LNB + m:SM_LNB + m + 1],
                        scale=sm[:, SM_LNG + m:SM_LNG + m + 1]), [("TT", "dbuf", db), ("sm",)], [("R", "mixT", m, s)])
            ic, sc = take(("ev_w_in", 0, 1536))
            ib, sbv = take(("ev_w_in", 0, 2048))
            for m in range(4):
                cb = cgbuf[m % 2]
                cbk = ("TT", "cgbuf", m % 2)
                if half == 0:
                    small("pool", lambda e, cb=cb: e.memset(cb[:, 0:2], 0.0), [], [cbk + ("halo",)])
                else:
                    small("pool", lambda e, cb=cb, m=m: e.tensor_copy(out=cb[:, 0:2], in_=halo_c[:, m, :]),
                          [("halo_c", m)], [cbk + ("halo",)])
                for s in range(2):
                    bc = fm_group(sc, m, s)
                    bvv = fm_group(sbv, m, s)
                    sgb = (2 * m + s) % 2
                    small("act", lambda e, bc=bc, sgb=sgb: e.activation(out=sig[sgb], in_=banks[bc][:], func=AF.Copy),
                          [("ps", bc)], [("TT", "sig", sgb)])
                    small("dve", lambda e, bvv=bvv, sgb=sgb, cb=cb, s=s: e.tensor_tensor(
                        out=cb[:, 2 + s * 512:2 + (s + 1) * 512], in0=banks[bvv][:], in1=sig[sgb], op=ALU.mult),
                        [("ps", bvv), ("TT", "sig", sgb)], [cbk + (s,)])
                    drain(6)
                allcb = [cbk + ("halo",), cbk + (0,), cbk + (1,)]
                small("pool", lambda e, cb=cb, m=m: e.tensor_copy(out=halo_c[:, m, :], in_=cb[:, T:T + 2]),
                      [cbk + (1,)], [("halo_c", m)])
                wc = lambda m, k: sm[:, SM_CB + m * 3 + k:SM_CB + m * 3 + k + 1]
                small("dve", lambda e, cb=cb, m=m: e.tensor_scalar(out=cconv[:, m, :], in0=cb[:, 2:2 + T], scalar1=wc(m, 2),
                                                                   scalar2=None, op0=ALU.mult),
                      allcb + [("sm",)], [("R", "cconv", m)])
                small("dve", lambda e, cb=cb, m=m: e.scalar_tensor_tensor(out=cconv[:, m, :], in0=cb[:, 1:1 + T], scalar=wc(m, 1),
                                                                          in1=cconv[:, m, :], op0=ALU.mult, op1=ALU.add),
                      allcb + [("sm",), ("R", "cconv", m)], [("R", "cconv", m)])
                small("dve", lambda e, cb=cb, m=m: e.scalar_tensor_tensor(out=cconv[:, m, :], in0=cb[:, 0:T], scalar=wc(m, 0),
                                                                          in1=cconv[:, m, :], op0=ALU.mult, op1=ALU.add),
                      allcb + [("sm",), ("R", "cconv", m)], [("R", "cconv", m)])
            drain(10 ** 6)
            release(ic)
            release(ib)
            i2, s2 = take(("ev_w_in", 0, 1024))
            for m in range(4):
                for s in range(2):
                    b = fm_group(s2, m, s)
                    small("dve", lambda e, b=b, m=m, s=s: e.tensor_tensor(
                        out=mixT[:, 4 + m, s * 512:(s + 1) * 512], in0=banks[b][:], in1=cconv[:, m, s * 512:(s + 1) * 512],
                        op=ALU.mult), [("ps", b), ("R", "cconv", m)], [("R", "mixT", 4 + m, s)])
            release(i2)
            out_proj("ev_w_out", mixT, lambda tb: [("R", "mixT", c, tb // 4) for c in range(8)], make_tail("norm", 1))

        def mixer1(mt):
            norm_to_xnT(2)
            while late_issue:
                issue(late_issue.pop(0))
            while held:
                release(held.pop(0))
            i0, s0 = take(("od_w_in", 0, 1024))
            i1, s1 = take(("od_w_in", 0, 1536))
            def v_stage(tb):
                if tb >= 3:
                    flush_deferred()
                vb = tb % 2
                c0 = stcol(8)
                for d, slot in ((0, s0), (1, s1)):
                    b = tm_group(slot, lambda kc, tb: xnT[:, kc, tb * 128:(tb + 1) * 128], [("xnT", tb)], tb)
                    small("dve", lambda e, b=b, d=d: e.tensor_tensor(out=vtmp[d], in0=banks[b][:],
                                                                     in1=cst[:, 1024 + d * 512:1024 + (d + 1) * 512], op=ALU.add),
                          [("ps", b), ("cst",)], [("TT", "vtmp", d)])
                    small("act", lambda e, d=d, vb=vb, c0=c0: e.activation(
                        out=vbuf[vb][:, d * 512:(d + 1) * 512], in_=vtmp[d], func=AF.Gelu, accum_out=st[:, c0 + d:c0 + d + 1]),
                        [("TT", "vtmp", d)], [("TT", "vbuf", vb, d), ("st", c0 + d)])
                vk = [("TT", "vbuf", vb, 0), ("TT", "vbuf", vb, 1)]
                small("act", lambda e, vb=vb, c0=c0, tb=tb: e.activation(out=vhat[:, tb, :], in_=vbuf[vb], func=AF.Square,
                                                                          accum_out=st[:, c0 + 2:c0 + 3]),
                      vk, [("st", c0 + 2), ("R", "vhat", tb)])
                col = lambda k, c0=c0: st[:, c0 + k:c0 + k + 1]
                small("dve", lambda e, col=col: e.tensor_tensor(out=col(3), in0=col(0), in1=col(1), op=ALU.add),
                      [("st", c0), ("st", c0 + 1)], [("st", c0 + 3)])
                small("dve", lambda e, col=col: e.tensor_scalar(out=col(3), in0=col(3), scalar1=1.0 / 1024, scalar2=None,
                                                                op0=ALU.mult), [("st", c0 + 3)], [("st", c0 + 3)])
                small("dve", lambda e, col=col: e.tensor_tensor(out=col(4), in0=col(3), in1=col(3), op=ALU.mult),
                      [("st", c0 + 3)], [("st", c0 + 4)])
                small("dve", lambda e, col=col: e.scalar_tensor_tensor(out=col(4), in0=col(2), scalar=1.0 / 1024, in1=col(4),
                                                                       op0=ALU.mult, op1=ALU.subtract),
                      [("st", c0 + 2), ("st", c0 + 4)], [("st", c0 + 4)])
                small("dve", lambda e, col=col: e.tensor_scalar(out=col(5), in0=col(4), scalar1=LN_EPS, scalar2=None,
                                                                op0=ALU.add), [("st", c0 + 4)], [("st", c0 + 5)])
                small("pool", lambda e, col=col: e.tensor_tensor(out=col(5), in0=col(5), in1=mhalf[:, 0:1], op=ALU.pow),
                      [("st", c0 + 5), ("mhalf",)], [("st", c0 + 5)])
                small("dve", lambda e, col=col: e.scalar_tensor_tensor(out=col(6), in0=col(3), scalar=-1.0, in1=col(5),
                                                                       op0=ALU.mult, op1=ALU.mult),
                      [("st", c0 + 3), ("st", c0 + 5)], [("st", c0 + 6)])
                small("act", lambda e, vb=vb, tb=tb, col=col: e.activation(out=vhat[:, tb, :], in_=vbuf[vb], func=AF.Identity,
                                                                           bias=col(6), scale=col(5)),
                      vk + [("st", c0 + 5), ("st", c0 + 6)], [("R", "vhat", tb)])
            def sp_stage(tb):
                for hb in range(2):
                    b = next_bank()

                    def smm(e, b=b, hb=hb, tb=tb):
                        for j in range(4):
                            g = hb * 4 + j
                            r = e.matmul(banks[b][:, j * 128:(j + 1) * 128], lhsT=vhat[:, tb, g * 128:(g + 1) * 128],
                                         rhs=wsT_bf[:, g, :], start=True, stop=True)
                        return r
                    P.op("pe", smm, reads=[("R", "vhat", tb), ("wsT_bf",)], writes=[("ps", b)])
                    sb_ = (tb % 2) * 2 + hb
                    sv3 = svtmp[sb_].rearrange("p (g t) -> p g t", g=4)
                    if hb == 0:
                        def gain_act(e, b=b, hb=hb, sv3=sv3):
                            for j in range(4):
                                r = e.activation(out=sv3[:, j, :], in_=banks[b][:, j * 128:(j + 1) * 128], func=AF.Copy,
                                                 scale=sm[:, SM_LNGV + hb * 4 + j:SM_LNGV + hb * 4 + j + 1])
                            return r
                        P.op("act", gain_act, reads=[("ps", b), ("sm",)], writes=[("TT", "svtmp", sb_)])
                    else:
                        small("dve", lambda e, b=b, hb=hb, sv3=sv3: e.tensor_tensor(
                            out=sv3, in0=banks[b][:].rearrange("p (g t) -> p g t", g=4),
                            in1=sm[:, SM_LNGV + hb * 4:SM_LNGV + hb * 4 + 4].unsqueeze(2).to_broadcast([128, 4, 128]),
                            op=ALU.mult), [("ps", b), ("sm",)], [("TT", "svtmp", sb_)])
                    small("dve", lambda e, hb=hb, sv3=sv3: e.tensor_tensor(
                        out=sv3, in0=sv3, in1=biasT[:, hb * 4:hb * 4 + 4, :], op=ALU.add),
                        [("TT", "svtmp", sb_), ("biasT", hb)], [("TT", "svtmp", sb_)])
                    small("pool", lambda e, hb=hb, sv3=sv3, tb=tb: e.tensor_tensor(
                        out=yT[:, hb * 4:hb * 4 + 4, tb * 128:(tb + 1) * 128], in0=sv3,
                        in1=uT[:, hb * 4:hb * 4 + 4, tb * 128:(tb + 1) * 128], op=ALU.mult),
                        [("TT", "svtmp", sb_)] + [("R", "uT", hb * 4 + j, tb // 4) for j in range(4)],
                        [("R", "yT", tb, hb)])
            for tb in range(NTB):
                v_stage(tb)
            release(i0)
            release(i1)
            for j in range(2):
                i, slot = take(("od_w_in", 0, j * 512))
                for m in range(4):
                    c = j * 4 + m
                    for s in range(2):
                        b = fm_group(slot, m, s)
                        small("act", lambda e, b=b, c=c, s=s: e.activation(
                            out=uT[:, c, s * 512:(s + 1) * 512], in_=banks[b][:], func=AF.Gelu,
                            bias=sm[:, SM_BU + c:SM_BU + c + 1], scale=1.0), [("ps", b), ("sm",)], [("R", "uT", c, s)])
                release(i)
            for tb in range(NTB):
                sp_stage(tb)
            out_proj("od_w_out", yT, lambda tb: [("R", "yT", tb, 0), ("R", "yT", tb, 1)], make_tail("norm", 3))

        for tb in range(NTB):
            load_x(0, tb)
        for mt in range(n_mt):
            for li, L in enumerate(layers):
                if L == 0:
                    mixer0(mt)
                else:
                    mixer1(mt)
                P.fence("R")
                P.fence("TT")
                if li + 1 < len(layers):
                    tail = make_tail("norm", 2 if layers[li + 1] == 1 else 0)
                elif final_norm:
                    tail = make_tail("final", mt=mt)
                else:
                    tail = make_tail(None)
                hold_final[0] = (li + 1 == len(layers)) and final_norm and (mt + 1 < n_mt)
                mlp(L, tail)
                hold_final[0] = False
                P.fence("R")
                P.fence("TT")
            if not final_norm:
                for tb in range(NTB):
                    dst = y[mt * T + tb * 128:mt * T + (tb + 1) * 128, :]
                    P.op("sp", lambda e, tb=tb, dst=dst: e.dma_start(out=dst, in_=h[:, tb, :]), reads=h_keys(tb), dma="out")
                    if mt + 1 < n_mt:
                        load_x(mt + 1, tb)
            P.fence("TT")
        assert cur[0] == len(pieces)
        cnt = P.emit(nc, final_waits=["out"])
        build_program.info = dict(signals=cnt, nops=len(P.ops), nsem=P.nsem)
    return nc


def _pack_inputs(inp):
    f = lambda a: np.ascontiguousarray(a, dtype=np.float32)
    pc = lambda v, n: np.asarray(v, np.float32).reshape(n, 128).T
    sm = np.zeros((128, SM_N), np.float32)
    gs = [inp["ev_norm_g"][0], inp["mlp_norm_g"][0], inp["od_norm_g"][0], inp["mlp_norm_g"][1]]
    for i, g in enumerate(gs):
        sm[:, SM_G + 8 * i:SM_G + 8 * i + 8] = pc(g, 8)
    caw = np.asarray(inp["ev_conv_a_w"][0], np.float32)
    sm[:, SM_CA:SM_CA + 124] = caw.T.reshape(4, 128, 31).transpose(1, 0, 2).reshape(128, 124)
    sm[:, SM_CAB:SM_CAB + 4] = pc(inp["ev_conv_a_b"][0], 4)
    sm[:, SM_LNG:SM_LNG + 4] = pc(inp["ev_ln_a_g"][0], 4)
    sm[:, SM_LNB:SM_LNB + 4] = pc(inp["ev_ln_a_b"][0], 4)
    cbw = np.asarray(inp["ev_conv_b_w"][0], np.float32)
    sm[:, SM_CB:SM_CB + 12] = cbw.T.reshape(4, 128, 3).transpose(1, 0, 2).reshape(128, 12)
    b_in = np.asarray(inp["od_b_in"][0], np.float32)
    sm[:, SM_BU:SM_BU + 8] = pc(b_in[:1024], 8)
    sm[:, SM_LNGV:SM_LNGV + 8] = pc(inp["od_ln_v_g"][0], 8)
    sm[:, SM_LNBV:SM_LNBV + 8] = pc(inp["od_ln_v_b"][0], 8)
    big = np.zeros((128, 4096), np.float32)
    big[:, 0:1024] = np.asarray(inp["final_norm_g"], np.float32)[None, :]
    big[:, 1024:2048] = b_in[None, 1024:]
    ws = np.asarray(inp["od_w_s"][0], np.float32)
    big[:, 2048:3072] = ws.transpose(2, 0, 1).reshape(128, 1024)
    big[:, 3072:4096] = np.asarray(inp["od_b_s"][0], np.float32).reshape(1, 1024)
    com = {
        "ev_w_in": f(inp["ev_w_in"][0]), "ev_w_out": f(inp["ev_w_out"][0]),
        "od_w_in": f(inp["od_w_in"][0]), "od_w_out": f(inp["od_w_out"][0]),
        "w1_0": f(inp["mlp_w1"][0]), "w1_1": f(inp["mlp_w1"][1]),
        "w2_0": f(inp["mlp_w2"][0]), "w2_1": f(inp["mlp_w2"][1]),
        "smallp": sm, "bigc": big,
    }
    return com


_CACHE = {}


def _prog(layers, final_norm):
    k = (tuple(layers), final_norm)
    if k not in _CACHE:
        _CACHE[k] = build_program(layers, final_norm)
    return _CACHE[k]


FUSED = True


def kernel(**inputs):
    com = _pack_inputs(inputs)
    x = np.ascontiguousarray(np.asarray(inputs["x"], np.float32)).reshape(NCORES, TOK, 1024)
    cores = list(range(NCORES))
    stages = [((0, 1), True)] if FUSED else [((0,), False), ((1,), True)]
    cur = x
    for layers, fin in stages:
        nc = _prog(layers, fin)
        in_maps = [dict(com, x=np.ascontiguousarray(cur[c])) for c in cores]
        res = run_bass_kernel_spmd(nc, in_maps, core_ids=cores)
        cur = np.stack([res.results[c]["y"] for c in cores], axis=0)
    return cur.reshape(16, 2048, 1024).astype(np.float32)
```
